# Optimizing a Trainium2 kernel written in Bass

```python
import jax, jax.numpy as jnp
from jax import lax
import numpy as np

D_MODEL = 2048
BATCH = 1
SEQ = 8192
DEPTH = 4

GRID_W = 64
CTX_LEN = 256

N_BRANCH = 4
BRANCH_W = 512
HEAD_DIM = 64
NA_HEADS = BRANCH_W // HEAD_DIM
NA_WIN_ROWS = 8
NA_WIN_COLS = 16
POOL_SIZES = (2, 4, 8, 16)
POOL_GROUP = BRANCH_W // len(POOL_SIZES)
GQA_Q_HEADS = BRANCH_W // HEAD_DIM
GQA_KV_HEADS = 2
Q_BLOCK = 128
ROPE_THETA = 10000.0
CONV_WIDTH = 31
EPS = 1e-6

PROJ_PARTS = (
    ("a_k", BRANCH_W), ("a_v", BRANCH_W),
    ("c_k", GQA_KV_HEADS * HEAD_DIM), ("c_v", GQA_KV_HEADS * HEAD_DIM),
    ("a_q", BRANCH_W), ("c_q", BRANCH_W),
    ("a_gate", BRANCH_W), ("b_in", BRANCH_W), ("b_gate", BRANCH_W),
    ("c_gate", BRANCH_W), ("d_glu", 2 * BRANCH_W), ("d_gate", BRANCH_W),
    ("merge", N_BRANCH * D_MODEL),
)
KV_COLS = 2 * BRANCH_W + 2 * GQA_KV_HEADS * HEAD_DIM
IN_COLS = sum(w for _, w in PROJ_PARTS)

kernel_name = "hybrid_parallel_gated_diffusion_block"


def split_proj(p):
    out = {}
    off = 0
    for name, w in PROJ_PARTS:
        if off + w > p.shape[-1]:
            break
        out[name] = p[..., off:off + w]
        off += w
    return out


def rms_norm(x, g):
    xf = x.astype(jnp.float32)
    y = xf * lax.rsqrt(jnp.mean(xf * xf, axis=-1, keepdims=True) + EPS)
    return (y * g).astype(x.dtype)


def layer_norm(x, g, b):
    xf = x.astype(jnp.float32)
    mu = jnp.mean(xf, axis=-1, keepdims=True)
    var = jnp.mean(jnp.square(xf - mu), axis=-1, keepdims=True)
    return ((xf - mu) * lax.rsqrt(var + EPS) * g + b).astype(x.dtype)


def modulate(x, g, shift, scale):
    return rms_norm(x, g) * (1 + scale) + shift


def _rope_axis(x, pos):
    half = x.shape[-1] // 2
    freqs = ROPE_THETA ** (-jnp.arange(half, dtype=jnp.float32) / half)
    ang = pos.astype(jnp.float32)[:, None] * freqs[None, :]
    cos = jnp.cos(ang)[:, None, :]
    sin = jnp.sin(ang)[:, None, :]
    xf = x.astype(jnp.float32)
    x1, x2 = xf[..., :half], xf[..., half:]
    return jnp.concatenate([x1 * cos - x2 * sin, x2 * cos + x1 * sin], axis=-1).astype(x.dtype)


def rope_2d(x, rows, cols):
    a = x.shape[-1] // 2
    return jnp.concatenate([_rope_axis(x[..., :a], rows), _rope_axis(x[..., a:], cols)], axis=-1)


def neighbourhood_attention(q, k, v, k_ctx, v_ctx, rpb):
    B, S, H, hd = q.shape
    R = S // GRID_W
    wr = min(NA_WIN_ROWS, R)
    r = jnp.arange(R)
    row_start = jnp.clip(r - wr // 2, 0, R - wr)
    band_rows = row_start[:, None] + jnp.arange(wr)[None, :]
    cq = jnp.arange(GRID_W)
    col_start = jnp.clip(cq - NA_WIN_COLS // 2, 0, GRID_W - NA_WIN_COLS)
    kc = jnp.arange(GRID_W)
    col_ok = (kc[None, :] >= col_start[:, None]) & (kc[None, :] < col_start[:, None] + NA_WIN_COLS)
    row_off = band_rows - r[:, None] + (NA_WIN_ROWS - 1)
    col_off = jnp.clip(kc[None, :] - cq[:, None], -(NA_WIN_COLS - 1), NA_WIN_COLS - 1) + (NA_WIN_COLS - 1)
    bias = rpb[:, row_off[:, None, :, None], col_off[None, :, None, :]].astype(jnp.float32)
    bias = jnp.where(col_ok[None, None, :, None, :], bias, -jnp.inf)
    scale = hd ** -0.5
    qg = q.reshape(B, R, GRID_W, H, hd)
    kg = k.reshape(B, R, GRID_W, H, hd)[:, band_rows]
    vg = v.reshape(B, R, GRID_W, H, hd)[:, band_rows]
    s_band = jnp.einsum('brqhd,brjkhd->bhrqjk', qg, kg).astype(jnp.float32) * scale + bias[None]
    s_ctx = jnp.einsum('brqhd,bchd->bhrqc', qg, k_ctx).astype(jnp.float32) * scale
    nb = wr * GRID_W
    s = jnp.concatenate([s_band.reshape(B, H, R, GRID_W, nb), s_ctx], axis=-1)
    p = jax.nn.softmax(s, axis=-1).astype(v.dtype)
    p_band = p[..., :nb].reshape(B, H, R, GRID_W, wr, GRID_W)
    p_ctx = p[..., nb:]
    o = jnp.einsum('bhrqjk,brjkhd->brqhd', p_band, vg) + jnp.einsum('bhrqc,bchd->brqhd', p_ctx, v_ctx)
    return o.reshape(B, S, H, hd)


def dense_attention(q, k, v):
    B, L, Hq, hd = q.shape
    Hk = k.shape[2]
    qg = q.reshape(B, L, Hk, Hq // Hk, hd)
    s = jnp.einsum('bqkgd,bnkd->bkgqn', qg, k).astype(jnp.float32) * (hd ** -0.5)
    p = jax.nn.softmax(s, axis=-1).astype(v.dtype)
    return jnp.einsum('bkgqn,bnkd->bqkgd', p, v).reshape(B, L, Hq, hd)


def gqa_latent_attention(q, k, v, k_ctx, v_ctx):
    B, S, Hq, hd = q.shape
    k_all = jnp.concatenate([k, k_ctx], axis=1)
    v_all = jnp.concatenate([v, v_ctx], axis=1)
    nblk = S // Q_BLOCK
    qb = jnp.moveaxis(q.reshape(B, nblk, Q_BLOCK, Hq, hd), 1, 0)
    o = lax.map(lambda qi: dense_attention(qi, k_all, v_all), qb)
    return jnp.moveaxis(o, 0, 1).reshape(B, S, Hq, hd)


def multiscale_pool(u, w_pool, pool_scale):
    B, L, W = u.shape
    uf = u.astype(jnp.float32)
    csum = jnp.concatenate([jnp.zeros((B, 1, W), jnp.float32), jnp.cumsum(uf, axis=1)], axis=1)
    t = jnp.arange(L)
    outs = []
    for gi, ksz in enumerate(POOL_SIZES):
        sl = slice(gi * POOL_GROUP, (gi + 1) * POOL_GROUP)
        lo = jnp.clip(t - ksz // 2, 0, L - 1)
        hi = jnp.clip(t + ksz - 1 - ksz // 2, 0, L - 1)
        cg = csum[..., sl]
        mean = (cg[:, hi + 1] - cg[:, lo]) / (hi - lo + 1).astype(jnp.float32)[None, :, None]
        d = (mean - uf[..., sl]).astype(u.dtype)
        outs.append(d @ w_pool[gi])
    return jnp.concatenate(outs, axis=-1) * pool_scale


def conformer_conv(glu_in, conv_w, conv_b, ln_g, ln_b, w_pw):
    a, g = jnp.split(glu_in, 2, axis=-1)
    u = a * jax.nn.sigmoid(g)
    y = lax.conv_general_dilated(
        u, conv_w[:, None, :].astype(u.dtype), window_strides=(1,),
        padding=((CONV_WIDTH // 2, CONV_WIDTH // 2),),
        dimension_numbers=('NWC', 'WIO', 'NWC'), feature_group_count=BRANCH_W) + conv_b
    y = jax.nn.silu(layer_norm(y, ln_g, ln_b))
    return y @ w_pw


def branch_merge(p, o_a, o_b, o_c, o_d, w_branch, w_out):
    outs = (o_a * jax.nn.silu(p['a_gate']), o_b * jax.nn.silu(p['b_gate']),
            o_c * jax.nn.silu(p['c_gate']), o_d * jax.nn.silu(p['d_gate']))
    gates = jax.nn.sigmoid(p['merge'])
    y = 0
    for bi in range(N_BRANCH):
        y = y + gates[..., bi * D_MODEL:(bi + 1) * D_MODEL] * (outs[bi] @ w_branch[bi])
    return y @ w_out


def setup_inputs(seed: int = 0) -> dict:
    key = jax.random.key(seed)
    ks = jax.random.split(key, 24)
    f = jnp.float32
    nrm = lambda k, shape, s: jax.random.normal(k, shape, f) * s
    return {
        "x": nrm(ks[0], (BATCH, SEQ, D_MODEL), 1.0),
        "c": nrm(ks[1], (BATCH, D_MODEL), 1.0),
        "ctx": nrm(ks[2], (BATCH, CTX_LEN, D_MODEL), 1.0),
        "c_ctx": nrm(ks[3], (D_MODEL,), 1.0),
        "w_ada": nrm(ks[4], (DEPTH, D_MODEL, 3 * D_MODEL), D_MODEL ** -0.5),
        "b_ada": nrm(ks[5], (DEPTH, 3 * D_MODEL), 0.01),
        "g_pre": 1.0 + nrm(ks[6], (DEPTH, D_MODEL), 0.05),
        "g_post": 1.0 + nrm(ks[7], (DEPTH, D_MODEL), 0.05),
        "w_in": nrm(ks[8], (DEPTH, D_MODEL, IN_COLS), D_MODEL ** -0.5),
        "na_rpb": nrm(ks[9], (DEPTH, NA_HEADS, 2 * NA_WIN_ROWS - 1, 2 * NA_WIN_COLS - 1), 0.1),
        "pool_w": nrm(ks[10], (DEPTH, len(POOL_SIZES), POOL_GROUP, POOL_GROUP), POOL_GROUP ** -0.5),
        "pool_scale": 1.0 + nrm(ks[11], (DEPTH, BRANCH_W), 0.1),
        "q_norm": 1.0 + nrm(ks[12], (DEPTH, HEAD_DIM), 0.05),
        "k_norm": 1.0 + nrm(ks[13], (DEPTH, HEAD_DIM), 0.05),
        "conv_w": nrm(ks[14], (DEPTH, CONV_WIDTH, BRANCH_W), CONV_WIDTH ** -0.5),
        "conv_b": nrm(ks[15], (DEPTH, BRANCH_W), 0.01),
        "conv_ln_g": 1.0 + nrm(ks[16], (DEPTH, BRANCH_W), 0.05),
        "conv_ln_b": nrm(ks[17], (DEPTH, BRANCH_W), 0.01),
        "conv_pw": nrm(ks[18], (DEPTH, BRANCH_W, BRANCH_W), BRANCH_W ** -0.5),
        "w_branch": nrm(ks[19], (DEPTH, N_BRANCH, BRANCH_W, D_MODEL), BRANCH_W ** -0.5),
        "w_out": nrm(ks[20], (DEPTH, D_MODEL, D_MODEL), D_MODEL ** -0.5),
    }


def reference(x, c, ctx, c_ctx, w_ada, b_ada, g_pre, g_post, w_in, na_rpb, pool_w, pool_scale,
              q_norm, k_norm, conv_w, conv_b, conv_ln_g, conv_ln_b, conv_pw, w_branch, w_out):
    B, S, _ = x.shape
    Cn = ctx.shape[1]
    t = jnp.arange(S)
    rows = t // GRID_W
    cols = t % GRID_W
    hd = HEAD_DIM
    for l in range(DEPTH):
        last = l == DEPTH - 1
        shift, scale, gate = jnp.split(jax.nn.silu(c) @ w_ada[l] + b_ada[l], 3, axis=-1)
        shift_c, scale_c, gate_c = jnp.split(jax.nn.silu(c_ctx) @ w_ada[l] + b_ada[l], 3, axis=-1)
        h = modulate(x, g_pre[l], shift[:, None], scale[:, None])
        hc = modulate(ctx, g_pre[l], shift_c, scale_c)
        p = split_proj(h @ w_in[l])
        pc = split_proj(hc @ (w_in[l, :, :KV_COLS] if last else w_in[l]))
        ka_c = pc['a_k'].reshape(B, Cn, NA_HEADS, hd)
        va_c = pc['a_v'].reshape(B, Cn, NA_HEADS, hd)
        kc_c = rms_norm(pc['c_k'].reshape(B, Cn, GQA_KV_HEADS, hd), k_norm[l])
        vc_c = pc['c_v'].reshape(B, Cn, GQA_KV_HEADS, hd)
        o_a = neighbourhood_attention(p['a_q'].reshape(B, S, NA_HEADS, hd), p['a_k'].reshape(B, S, NA_HEADS, hd),
                                      p['a_v'].reshape(B, S, NA_HEADS, hd), ka_c, va_c, na_rpb[l])
        o_b = multiscale_pool(p['b_in'], pool_w[l], pool_scale[l])
        qc = rope_2d(rms_norm(p['c_q'].reshape(B, S, GQA_Q_HEADS, hd), q_norm[l]), rows, cols)
        kc = rope_2d(rms_norm(p['c_k'].reshape(B, S, GQA_KV_HEADS, hd), k_norm[l]), rows, cols)
        vc = p['c_v'].reshape(B, S, GQA_KV_HEADS, hd)
        o_c = gqa_latent_attention(qc, kc, vc, kc_c, vc_c)
        o_d = conformer_conv(p['d_glu'], conv_w[l], conv_b[l], conv_ln_g[l], conv_ln_b[l], conv_pw[l])
        y = branch_merge(p, o_a.reshape(B, S, BRANCH_W), o_b, o_c.reshape(B, S, BRANCH_W), o_d,
                         w_branch[l], w_out[l])
        x_next = x + gate[:, None] * rms_norm(y, g_post[l])
        if not last:
            o_a_c = dense_attention(pc['a_q'].reshape(B, Cn, NA_HEADS, hd), ka_c, va_c)
            o_b_c = multiscale_pool(pc['b_in'], pool_w[l], pool_scale[l])
            qc_c = rms_norm(pc['c_q'].reshape(B, Cn, GQA_Q_HEADS, hd), q_norm[l])
            o_c_c = dense_attention(qc_c, kc_c, vc_c)
            o_d_c = conformer_conv(pc['d_glu'], conv_w[l], conv_b[l], conv_ln_g[l], conv_ln_b[l], conv_pw[l])
            y_c = branch_merge(pc, o_a_c.reshape(B, Cn, BRANCH_W), o_b_c, o_c_c.reshape(B, Cn, BRANCH_W),
                               o_d_c, w_branch[l], w_out[l])
            ctx = ctx + gate_c * rms_norm(y_c, g_post[l])
        x = x_next
    return x
```

```python
import contextlib
import numpy as np
import ml_dtypes
import concourse.bass as bass
import concourse.mybir as mybir
from concourse.bass_utils import run_bass_kernel_spmd

F32 = mybir.dt.float32
BF16 = mybir.dt.bfloat16
AF = mybir.ActivationFunctionType
ALU = mybir.AluOpType

D = 2048
KC = 16
NL = 1024
NCX = 256
NT = 1280
SMALL = 5888
TCH = [(0, 512), (512, 512), (1024, 256)]
EPS = 1e-6
NEG = -30000.0
SHIFT_NA = 0.0
SHIFT_GQA = 0.0


class R:
    __slots__ = ("w", "r")

    def __init__(self):
        self.w = None
        self.r = {}


class Prog:
    CE = ("pe", "act", "dve", "pool")
    QS = ("sp", "act", "pool")
    NRING = 12

    def __init__(self, nc):
        self.nc = nc
        self.streams = {e: [] for e in ("pe", "act", "dve", "pool", "sp")}
        self.cnt = {e: 0 for e in self.CE}
        self.known = {e: {} for e in self.streams}
        self.dma_i = {q: 0 for q in self.QS}
        self.sems = {}
        self.semnames = list(self.CE) + [f"d{q}{i}" for q in self.QS for i in range(self.NRING)]

    def _collect(self, reads, writes):
        need = {}
        for r in reads:
            if r.w is not None and need.get(r.w[0], 0) < r.w[1]:
                need[r.w[0]] = r.w[1]
        for w in writes:
            if w.w is not None and need.get(w.w[0], 0) < w.w[1]:
                need[w.w[0]] = w.w[1]
            for k, v in w.r.items():
                if need.get(k, 0) < v:
                    need[k] = v
        return need

    def _emit_waits(self, eng, need, skip_key=None):
        kn = self.known[eng]
        for k, v in need.items():
            if k == skip_key:
                continue
            if kn.get(k, 0) < v:
                kn[k] = v
                self.streams[eng].append(("wait", k, v))

    def op(self, eng, fn, reads=(), writes=(), nosync_self=False):
        need = self._collect(reads, writes)
        self._emit_waits(eng, need, skip_key=eng if nosync_self else None)
        self.cnt[eng] += 1
        n = self.cnt[eng]
        self.streams[eng].append(("op", fn, eng, 1))
        for r in reads:
            r.r[eng] = n
        for w in writes:
            w.w = (eng, n)
            w.r = {}
        return n

    def dma(self, q, fn, reads=(), writes=()):
        need = self._collect(reads, writes)
        i = self.dma_i[q]
        self.dma_i[q] += 1
        key = f"d{q}{i % self.NRING}"
        prev = 16 * (i // self.NRING)
        if prev > 0 and need.get(key, 0) < prev:
            need[key] = prev
        self._emit_waits(q, need)
        val = prev + 16
        self.streams[q].append(("op", fn, key, 16))
        for r in reads:
            r.r[key] = val
        for w in writes:
            w.w = (key, val)
            w.r = {}

    def _all_done(self):
        need = {ce: self.cnt[ce] for ce in self.CE if self.cnt[ce] > 0}
        for q in self.QS:
            n = self.dma_i[q]
            for i in range(max(0, n - self.NRING), n):
                need[f"d{q}{i % self.NRING}"] = 16 * (i // self.NRING + 1)
        return need

    def barrier(self):
        need = self._all_done()
        for eng in self.streams:
            self._emit_waits(eng, need)

    def finish(self):
        self._emit_waits("sp", self._all_done())

    def build(self):
        nc = self.nc
        with contextlib.ExitStack() as es:
            for name in self.semnames:
                self.sems[name] = es.enter_context(nc.semaphore(name))
            block = es.enter_context(nc.Block())
            sems = self.sems

            def run(stream):
                def f(eng):
                    for it in stream:
                        if it[0] == "wait":
                            eng.wait_ge(sems[it[1]], it[2])
                        else:
                            it[1](eng).then_inc(sems[it[2]], it[3])
                return f
            block.tensor(run(self.streams["pe"]))
            block.scalar(run(self.streams["act"]))
            block.vector(run(self.streams["dve"]))
            block.gpsimd(run(self.streams["pool"]))
            block.sync(run(self.streams["sp"]))


class Arena:
    def __init__(self, tensor, words):
        self.t = tensor
        self.cap = words
        self.off = 0

    def reset(self):
        self.off = 0

    def alloc(self, free_shape, dt=F32, parts=128):
        n = int(np.prod(free_shape))
        words = n if dt == F32 else (n + 1) // 2
        words = (words + 7) // 8 * 8
        start = self.off
        self.off += words
        assert self.off <= self.cap, f"arena overflow {self.off} > {self.cap}"
        ap = self.t[0:parts, start:start + (n if dt == F32 else (n + 1) // 2)]
        if dt != F32:
            ap = ap.bitcast(dt)[:, 0:n]
        if len(free_shape) == 2:
            ap = ap.rearrange("p (a b) -> p a b", a=free_shape[0])
        elif len(free_shape) == 3:
            ap = ap.rearrange("p (a b c) -> p a b c", a=free_shape[0], b=free_shape[1])
        return ap


class Ring:
    def __init__(self, tiles):
        self.tiles = tiles
        self.rs = [R() for _ in tiles]
        self.i = 0

    def next(self):
        i = self.i % len(self.tiles)
        self.i += 1
        return self.tiles[i], self.rs[i]


class H:
    def __init__(self, P):
        self.P = P
        self.flip = 0

    def mm(self, out, lhsT, rhs, start, stop, reads, writes):
        self.P.op("pe", lambda e: e.matmul(out, lhsT=lhsT, rhs=rhs, start=start, stop=stop),
                  reads=reads, writes=writes, nosync_self=True)

    def tr(self, out, in_, ident, reads, writes):
        self.P.op("pe", lambda e: e.transpose(out=out, in_=in_, identity=ident),
                  reads=reads, writes=writes, nosync_self=True)

    def act(self, out, in_, func, reads, writes, bias=None, scale=None, accum_out=None):
        kw = {}
        if bias is not None:
            kw["bias"] = bias
        if scale is not None:
            kw["scale"] = scale
        if accum_out is not None:
            kw["accum_out"] = accum_out
        self.P.op("act", lambda e: e.activation(out=out, in_=in_, func=func, **kw), reads=reads, writes=writes)

    def copy(self, eng, out, in_, reads, writes):
        if eng == "act":
            self.P.op("act", lambda e: e.copy(out=out, in_=in_), reads=reads, writes=writes)
        else:
            self.P.op(eng, lambda e: e.tensor_copy(out=out, in_=in_), reads=reads, writes=writes)

    def anycopy(self, out, in_, reads, writes):
        self.flip ^= 1
        self.copy("act" if self.flip else "dve", out, in_, reads, writes)

    def tt(self, eng, out, in0, in1, op, reads, writes):
        self.P.op(eng, lambda e: e.tensor_tensor(out=out, in0=in0, in1=in1, op=op), reads=reads, writes=writes)

    def ts(self, eng, out, in0, s1, s2, op0, op1, reads, writes):
        if s2 is None:
            self.P.op(eng, lambda e: e.tensor_scalar(out=out, in0=in0, scalar1=s1, scalar2=None, op0=op0),
                      reads=reads, writes=writes)
        else:
            self.P.op(eng, lambda e: e.tensor_scalar(out=out, in0=in0, scalar1=s1, scalar2=s2, op0=op0, op1=op1),
                      reads=reads, writes=writes)

    def stt(self, eng, out, in0, scalar, in1, op0, op1, reads, writes):
        self.P.op(eng, lambda e: e.scalar_tensor_tensor(out=out, in0=in0, scalar=scalar, in1=in1, op0=op0, op1=op1),
                  reads=reads, writes=writes)

    def recip(self, out, in_, reads, writes):
        self.P.op("dve", lambda e: e.reciprocal(out=out, in_=in_), reads=reads, writes=writes)

    def memset(self, eng, out, val, writes):
        self.P.op(eng, lambda e: e.memset(out, val), writes=writes)

    def dma(self, q, out, in_, reads, writes):
        self.P.dma(q, lambda e: e.dma_start(out=out, in_=in_), reads=reads, writes=writes)


def bcast_rows(ap2d, row, c0, n, parts=128):
    ncols = ap2d.shape[1]
    return bass.AP(ap2d.tensor, ap2d.offset + row * ncols + c0, [[0, parts], [1, n]])


def fm(ap2d):
    return ap2d.rearrange("(ch p) t -> p ch t", p=128)


BLK_TYPES = {}
for b in range(0, 4): BLK_TYPES[b] = ("copy", "KA", b)
for b in range(4, 8): BLK_TYPES[b] = ("tm", "VA", b - 4)
BLK_TYPES[8] = ("qk", "KC", 0)
BLK_TYPES[9] = ("tm", "VC", 0)
for b in range(10, 14): BLK_TYPES[b] = ("copy", "QA", b - 10)
for b in range(14, 18): BLK_TYPES[b] = ("qk", "QC", b - 14)
for b in range(18, 22): BLK_TYPES[b] = ("silu", "GA", b - 18)
for b in range(22, 26): BLK_TYPES[b] = ("copy", "BI", b - 22)
for b in range(26, 30): BLK_TYPES[b] = ("silu", "GB", b - 26)
for b in range(30, 34): BLK_TYPES[b] = ("silu", "GC", b - 30)
for b in range(34, 38): BLK_TYPES[b] = ("glu_a", "U", b - 34)
for b in range(38, 42): BLK_TYPES[b] = ("glu_g", "U", b - 38)
for b in range(42, 46): BLK_TYPES[b] = ("silu", "GD", b - 42)


def build_A():
    nc = bass.Bass("TRN2", target_bir_lowering=False)
    P = Prog(nc)
    h = H(P)

    def din(n, s, dt=F32):
        return nc.dram_tensor(n, s, dt, kind="ExternalInput").ap()

    def dout(n, s, dt=BF16):
        return nc.dram_tensor(n, s, dt, kind="ExternalOutput").ap()
    x = din("x", [NT, D])
    c2T = din("c2T", [128, KC * 2])
    w_ada = din("w_ada", [D, 3 * D])
    b_ada2 = din("b_ada2", [2, 3 * D])
    g_pre = din("g_pre", [1, D])
    w_in = din("w_in", [D, SMALL])
    consts = din("consts", [128, 386])
    cs = din("cs", [128, 2 * NL])
    MOD = dout("MOD", [2, 3 * D], F32)
    HT = dout("HT", [128, KC * NT])
    O = {n: dout(n, [512, NT]) for n in ("KA", "QA", "QC", "GA", "GB", "GC", "GD", "BI", "U")}
    O["KC"] = dout("KC", [128, NT])
    O["VA"] = dout("VA", [NT, 512])
    O["VC"] = dout("VC", [NT, 128])

    with contextlib.ExitStack() as es:
        arena_t = es.enter_context(nc.sbuf_tensor("arena", [128, 51200], F32))
        ar = Arena(arena_t, 51200)
        pst = [es.enter_context(nc.psum_tensor(f"ps{i}", [128, 512], F32)) for i in range(8)]
        psr = Ring([t[:] for t in pst])

        cst = ar.alloc([386]); r_cst = R()
        identb = ar.alloc([128], BF16); bdb = ar.alloc([128], BF16); rmb = ar.alloc([128], BF16); r_cb = R()
        hT = ar.alloc([KC, NT], BF16); r_hT = R()
        wring = Ring([ar.alloc([KC, 512], BF16) for _ in range(3)])
        eps_c = ar.alloc([1]); r_eps = R()
        mark = ar.off
        h.memset("dve", eps_c, EPS, [r_eps])

        h.dma("sp", cst, consts, [], [r_cst])
        h.copy("dve", identb, cst[:, 0:128], [r_cst], [r_cb])
        h.copy("dve", bdb, cst[:, 128:256], [r_cst], [r_cb])
        h.copy("dve", rmb, cst[:, 256:384], [r_cst], [r_cb])

        scf = ar.alloc([KC * 2]); scb = ar.alloc([KC, 2], BF16); r_sc = R()
        h.dma("sp", scf, c2T, [], [r_sc])
        h.act(scb.rearrange("p a b -> p (a b)"), scf, AF.Silu, [r_sc], [r_sc])
        b2ring = Ring([ar.alloc([512], F32, parts=2) for _ in range(2)])
        mring = Ring([ar.alloc([512], F32, parts=2) for _ in range(2)])
        r_MOD = R()
        for n in range(12):
            wt, wr = wring.next()
            P.dma("pool", (lambda wt, n: lambda e: e.dma_start(
                out=wt, in_=w_ada[:, n * 512:(n + 1) * 512].rearrange("(kc p) n -> p kc n", p=128)))(wt, n),
                writes=[wr])
            bt, br = b2ring.next()
            h.dma("sp", bt, b_ada2[:, n * 512:(n + 1) * 512], [], [br])
            pt, pr = psr.next()
            for kc in range(KC):
                h.mm(pt[0:2, :], scb[:, kc, :], wt[:, kc, :], kc == 0, kc == KC - 1, [r_sc, wr], [pr])
            mt, mr = mring.next()
            h.tt("dve", mt, pt[0:2, :], bt, ALU.add, [pr, br], [mr])
            h.dma("sp", MOD[:, n * 512:(n + 1) * 512], mt, [mr], [r_MOD])
        A1 = [ar.alloc([D]) for _ in range(2)]
        SH = [ar.alloc([D]) for _ in range(2)]
        r_A1 = [R(), R()]; r_SH = [R(), R()]
        gpb = ar.alloc([D]); r_gpb = R()
        h.dma("sp", gpb, bcast_rows(g_pre, 0, 0, D), [], [r_gpb])
        for r in range(2):
            h.dma("sp", SH[r], bcast_rows(MOD, r, 0, D), [r_MOD], [r_SH[r]])
            h.dma("sp", A1[r], bcast_rows(MOD, r, D, D), [r_MOD], [r_A1[r]])
            h.stt("dve", A1[r], A1[r], 1.0, gpb, ALU.add, ALU.mult, [r_A1[r], r_gpb], [r_A1[r]])
        xring = Ring([ar.alloc([D]) for _ in range(2)])
        tring = Ring([ar.alloc([D]) for _ in range(1)])
        hbring = Ring([ar.alloc([D], BF16) for _ in range(2)])
        junk = ar.alloc([D], BF16); r_junk = R()
        stat = Ring([ar.alloc([4]) for _ in range(2)])
        for tt in range(NT // 128):
            r = 0 if tt < 8 else 1
            xt, xr = xring.next()
            h.dma("sp", xt, x[tt * 128:(tt + 1) * 128, :], [], [xr])
            st, sr = stat.next()
            h.act(junk, xt, AF.Square, [xr], [r_junk, sr], accum_out=st[:, 0:1])
            h.act(st[:, 1:2], st[:, 0:1], AF.Sqrt, [sr, r_eps], [sr], bias=eps_c[:, 0:1], scale=1.0 / D)
            h.recip(st[:, 2:3], st[:, 1:2], [sr], [sr])
            t_, tr_ = tring.next()
            h.stt("dve", t_, xt, st[:, 2:3], A1[r], ALU.mult, ALU.mult, [xr, sr, r_A1[r]], [tr_])
            hb, hbr = hbring.next()
            h.tt("pool", hb, t_, SH[r], ALU.add, [tr_, r_SH[r]], [hbr])
            for j in range(4):
                pt, pr = psr.next()
                ptb = pt.bitcast(BF16)
                for q in range(4):
                    kc = j * 4 + q
                    h.tr(ptb[:, q * 128:(q + 1) * 128], hb[:, kc * 128:(kc + 1) * 128], identb, [hbr, r_cb], [pr])
                h.anycopy(hT[:, j * 4:(j + 1) * 4, tt * 128:(tt + 1) * 128],
                          ptb[:, 0:512].rearrange("p (a b) -> p a b", a=4), [pr], [r_hT])
        h.dma("sp", HT.rearrange("p (a b) -> p a b", a=KC), hT, [r_hT], [])
        P.barrier()
        ar.off = mark

        cs_sb = ar.alloc([2 * NL]); r_cs = R()
        h.dma("sp", cs_sb, cs, [], [r_cs])
        a_sb = ar.alloc([4, NT]); r_a = [R() for _ in range(4)]
        stg = Ring([ar.alloc([512], BF16) for _ in range(6)])
        tmp = Ring([ar.alloc([512]) for _ in range(8)])

        def epilogue(kind, dst, j, pt, pr, t0, tn):
            if kind in ("copy", "silu"):
                s, sr = stg.next()
                if kind == "copy":
                    h.anycopy(s[:, 0:tn], pt[:, 0:tn], [pr], [sr])
                else:
                    h.act(s[:, 0:tn], pt[:, 0:tn], AF.Silu, [pr], [sr])
                h.dma("sp", O[dst][j * 128:(j + 1) * 128, t0:t0 + tn], s[:, 0:tn], [sr], [])
            elif kind == "glu_a":
                h.copy("dve", a_sb[:, j, t0:t0 + tn], pt[:, 0:tn], [pr], [r_a[j]])
            elif kind == "glu_g":
                g, gr = tmp.next()
                h.act(g[:, 0:tn], pt[:, 0:tn], AF.Sigmoid, [pr], [gr])
                s, sr = stg.next()
                h.tt("dve", s[:, 0:tn], a_sb[:, j, t0:t0 + tn], g[:, 0:tn], ALU.mult, [gr, r_a[j]], [sr])
                h.dma("sp", O[dst][j * 128:(j + 1) * 128, t0:t0 + tn], s[:, 0:tn], [sr], [])
            elif kind == "qk":
                gcol = cst[:, 384:385] if dst == "QC" else cst[:, 385:386]
                sq, sqr = stg.next()
                h.act(sq[:, 0:tn], pt[:, 0:tn], AF.Square, [pr], [sqr])
                p2, p2r = psr.next()
                h.mm(p2[:, 0:tn], bdb, sq[:, 0:tn], True, True, [sqr, r_cb], [p2r])
                rs, rsr = tmp.next()
                h.act(rs[:, 0:tn], p2[:, 0:tn], AF.Sqrt, [p2r, r_eps], [rsr], bias=eps_c[:, 0:1], scale=1.0)
                h.recip(rs[:, 0:tn], rs[:, 0:tn], [rsr], [rsr])
                qn, qnr = tmp.next()
                h.stt("dve", qn[:, 0:tn], pt[:, 0:tn], gcol, rs[:, 0:tn], ALU.mult, ALU.mult, [pr, rsr, r_cst], [qnr])
                s, sr = stg.next()
                if t0 < NL:
                    qb, qbr = stg.next()
                    h.copy("act", qb[:, 0:tn], qn[:, 0:tn], [qnr], [qbr])
                    p3, p3r = psr.next()
                    h.mm(p3[:, 0:tn], rmb, qb[:, 0:tn], True, True, [qbr, r_cb], [p3r])
                    t1, t1r = tmp.next()
                    h.tt("dve", t1[:, 0:tn], qn[:, 0:tn], cs_sb[:, t0:t0 + tn], ALU.mult, [qnr, r_cs], [t1r])
                    t2, t2r = tmp.next()
                    h.tt("dve", t2[:, 0:tn], p3[:, 0:tn], cs_sb[:, NL + t0:NL + t0 + tn], ALU.mult, [p3r, r_cs], [t2r])
                    h.tt("pool", s[:, 0:tn], t1[:, 0:tn], t2[:, 0:tn], ALU.add, [t1r, t2r], [sr])
                else:
                    h.copy("act", s[:, 0:tn], qn[:, 0:tn], [qnr], [sr])
                h.dma("sp", O[dst][j * 128:(j + 1) * 128, t0:t0 + tn], s[:, 0:tn], [sr], [])

        ngroups = (SMALL + 511) // 512
        for g in range(ngroups):
            c0 = g * 512
            wcols = min(512, SMALL - c0)
            wt, wr = wring.next()
            P.dma("pool", (lambda wt, c0, wcols: lambda e: e.dma_start(
                out=wt[:, :, 0:wcols], in_=w_in[:, c0:c0 + wcols].rearrange("(kc p) n -> p kc n", p=128)))(wt, c0, wcols),
                writes=[wr])
            b = g * 4
            nb = wcols // 128
            bi = 0
            while bi < nb:
                kind, dst, j = BLK_TYPES[b + bi]
                if kind == "tm":
                    run = 1
                    while bi + run < nb and BLK_TYPES[b + bi + run][0] == "tm" and BLK_TYPES[b + bi + run][1] == dst:
                        run += 1
                    ncol = run * 128
                    co = bi * 128
                    for tt in range(NT // 128):
                        pt, pr = psr.next()
                        for kc in range(KC):
                            h.mm(pt[:, 0:ncol], hT[:, kc, tt * 128:(tt + 1) * 128], wt[:, kc, co:co + ncol],
                                 kc == 0, kc == KC - 1, [r_hT, wr], [pr])
                        s, sr = stg.next()
                        h.anycopy(s[:, 0:ncol], pt[:, 0:ncol], [pr], [sr])
                        h.dma("sp", O[dst][tt * 128:(tt + 1) * 128, j * 128:j * 128 + ncol], s[:, 0:ncol], [sr], [])
                    bi += run
                else:
                    co = bi * 128
                    for (t0, tn) in TCH:
                        pt, pr = psr.next()
                        for kc in range(KC):
                            h.mm(pt[:, 0:tn], wt[:, kc, co:co + 128], hT[:, kc, t0:t0 + tn],
                                 kc == 0, kc == KC - 1, [r_hT, wr], [pr])
                        epilogue(kind, dst, j, pt, pr, t0, tn)
                    bi += 1
        P.finish()
        P.build()
    return nc


NHALO = 23
KAW = NHALO * 64 + NCX
BIW = (NL + 16) + (NCX + 16)
UW = (NL + 30) + (NCX + 30)
NKC = (8 * NL + NCX) // 128


def build_B(phases=("gqa", "na", "pool", "conv", "merge", "out")):
    nc = bass.Bass("TRN2", target_bir_lowering=False)
    P = Prog(nc)
    h = H(P)

    def din(n, s, dt=F32):
        return nc.dram_tensor(n, s, dt, kind="ExternalInput").ap()
    x = din("x", [NT, D])
    GATE = din("GATE", [2, D])
    g_post = din("g_post", [1, D])
    HT = din("HT", [128, KC * NT], BF16)
    I = {n: din(n, [512, NT], BF16) for n in ("QA", "QC", "GA", "GB", "GC", "GD")}
    KCall = din("KCall", [128, NKC * 128], BF16)
    VCall = din("VCall", [NKC * 128, 128], BF16)
    KAh = din("KAh", [512, KAW], BF16)
    VAh = din("VAh", [KAW, 512], BF16)
    BIh = din("BIh", [512, BIW], BF16)
    Uh = din("Uh", [512, UW], BF16)
    Tg = din("Tg", [128, 4096])
    CMt = din("CMt", [128, 4096])
    RMK = din("RMK", [128, 42])
    ICN = din("ICN", [4, NT])
    w_m = din("w_m", [D, 4 * D])
    w_br = din("w_br", [4 * 512, D])
    w_out = din("w_out", [D, D])
    pool_w = din("pool_w", [512, 128])
    psc = din("psc", [128, 4])
    cw = din("cw", [128, 4 * 31])
    cvec = din("cvec", [128, 12])
    conv_pw = din("conv_pw", [512, 512])
    consts = din("consts", [128, 256])
    XN = nc.dram_tensor("XN", [NT, D], F32, kind="ExternalOutput").ap()
    OUTS = nc.dram_tensor("OUTS", [4 * 512, NT], BF16).ap()
    r_OUTS = [R() for _ in range(4)]

    with contextlib.ExitStack() as es:
        arena_t = es.enter_context(nc.sbuf_tensor("arena", [128, 51200], F32))
        ar = Arena(arena_t, 51200)
        pst = [es.enter_context(nc.psum_tensor(f"ps{i}", [128, 512], F32)) for i in range(8)]
        PS = [t[:] for t in pst]

        cst = ar.alloc([256]); r_cst = R()
        identb = ar.alloc([128], BF16); onesb = ar.alloc([128], BF16); r_cb = R()
        eps_c = ar.alloc([1]); r_eps = R()
        h.memset("dve", eps_c, EPS, [r_eps])
        h.dma("sp", cst, consts, [], [r_cst])
        h.copy("dve", identb, cst[:, 0:128], [r_cst], [r_cb])
        h.copy("dve", onesb, cst[:, 128:256], [r_cst], [r_cb])
        mark0 = ar.off

        if "gqa" in phases:
            QC_sb = ar.alloc([4, NT], BF16); r_qc = R()
            GC_sb = ar.alloc([4, NT], BF16); r_gc = R()
            oc_sb = ar.alloc([4, NT], BF16); r_oc = R()
            KC2 = ar.alloc([2, NKC * 128], BF16); r_kc2 = R()
            VCe = ar.alloc([NKC, 2, 128], BF16); r_vce = R()
            VCt = ar.alloc([NKC, 128], BF16); r_vct = R()
            h.dma("sp", QC_sb, fm(I["QC"]), [], [r_qc])
            h.dma("sp", GC_sb, fm(I["GC"]), [], [r_gc])
            for g in range(2):
                for hp in (0, 64):
                    h.dma("sp", KC2[hp:hp + 64, g, :], KCall[g * 64:(g + 1) * 64, :], [], [r_kc2])
            vsrc = VCall.rearrange("(kc p) c -> p kc c", p=128)
            for a in range(0, NKC, 11):
                h.dma("sp", VCt[:, a:a + 11, :], vsrc[:, a:a + 11, :], [], [r_vct])
            h.memset("pool", VCe.rearrange("p a b c -> p (a b c)"), 1.0, [r_vce])
            for g in range(2):
                h.copy("pool", VCe[:, :, g, 0:64], VCt[:, :, g * 64:(g + 1) * 64], [r_vct], [r_vce])
            ptring = Ring([ar.alloc([512], BF16) for _ in range(3)])
            rcring = Ring([ar.alloc([512]) for _ in range(2)])
            sring = Ring(PS[0:3])
            accring = Ring(PS[3:5])
            for h8 in range(8):
                g = h8 // 4; hp = (h8 % 2) * 64; ch = h8 // 2
                for (t0, tn) in TCH:
                    kcs = list(range(NKC)) if t0 < NL else [NKC - 2, NKC - 1]
                    acc, accr = accring.next()
                    pend = []

                    def pv(item, acc=acc, accr=accr, n=len(kcs), tn=tn, g=g):
                        idx, kc, pt, ptr = item
                        h.mm(acc[:, 0:tn], VCe[:, kc, g, :], pt[:, 0:tn], idx == 0, idx == n - 1, [r_vce, ptr], [accr])
                    for idx, kc in enumerate(kcs):
                        s, sr = sring.next()
                        h.mm(s[:, 0:tn], KC2[hp:hp + 64, g, kc * 128:(kc + 1) * 128], QC_sb[hp:hp + 64, ch, t0:t0 + tn],
                             True, True, [r_kc2, r_qc], [sr])
                        pt, ptr = ptring.next()
                        h.act(pt[:, 0:tn], s[:, 0:tn], AF.Exp, [sr], [ptr], scale=0.125)
                        pend.append((idx, kc, pt, ptr))
                        if len(pend) > 1:
                            pv(pend.pop(0))
                    while pend:
                        pv(pend.pop(0))
                    rc, rcr = rcring.next()
                    h.recip(rc[64:128, 0:tn], acc[64:128, 0:tn], [accr], [rcr])
                    h.tt("dve", oc_sb[hp:hp + 64, ch, t0:t0 + tn], acc[0:64, 0:tn], rc[64:128, 0:tn], ALU.mult,
                         [accr, rcr], [r_oc])
            h.tt("pool", oc_sb, oc_sb, GC_sb, ALU.mult, [r_oc, r_gc], [r_oc])
            h.dma("sp", fm(OUTS[2 * 512:3 * 512, :]), oc_sb, [r_oc], [r_OUTS[2]])
            P.barrier()
            ar.off = mark0

        if "na" in phases:
            QA_sb = ar.alloc([4, NT], BF16); r_qa = R()
            GA_sb = ar.alloc([4, NT], BF16); r_ga = R()
            oa_sb = ar.alloc([4, NT], BF16); r_oa = R()
            KA_sb = ar.alloc([4, KAW], BF16); r_ka = R()
            VAe = ar.alloc([24, 8, 128], BF16); r_vae = R()
            Tb = ar.alloc([8, 8, 64], BF16); r_tb = R()
            RMK_sb = ar.alloc([42]); r_rmk = R()
            ptring = Ring([ar.alloc([512], BF16) for _ in range(3)])
            rcring = Ring([ar.alloc([512]) for _ in range(2)])
            mark1 = ar.off
            VAt = ar.alloc([11, 512], BF16); r_vat = R()
            Tg_sb = ar.alloc([4096]); r_tg = R()
            CM_sb = ar.alloc([4096]); r_cm = R()
            h.dma("sp", QA_sb, fm(I["QA"]), [], [r_qa])
            h.dma("sp", GA_sb, fm(I["GA"]), [], [r_ga])
            h.dma("sp", KA_sb, fm(KAh), [], [r_ka])
            h.dma("sp", RMK_sb, RMK, [], [r_rmk])
            h.dma("sp", Tg_sb, Tg, [], [r_tg])
            h.dma("sp", CM_sb, CMt, [], [r_cm])
            h.stt("dve", Tb.rearrange("p a b c -> p (a b c)"), Tg_sb, 8.0, CM_sb, ALU.mult, ALU.add, [r_tg, r_cm], [r_tb])
            h.memset("pool", VAe.rearrange("p a b c -> p (a b c)"), 1.0, [r_vae])
            for (src0, npr, dst0) in ((0, 11, 0), (64, 11, 11), (NHALO * 64, 2, 22)):
                h.dma("sp", VAt[:, 0:npr, :], VAh[src0:src0 + npr * 128, :].rearrange("(j p) c -> p j c", p=128), [r_vae], [r_vat])
                for j in range(npr):
                    h.copy("pool" if j % 2 else "dve", VAe[:, dst0 + j, :, 0:64],
                           VAt[:, j, :].rearrange("p (a b) -> p a b", a=8), [r_vat], [r_vae])
            sring = Ring(PS[0:3])
            oring = Ring(PS[3:5])
            CT0 = NHALO * 64
            pend = []

            def na_pv(item):
                i, h8, chunks, pt, ptr, oacc, oaccr = item
                n = len(chunks)
                for c, vidx in enumerate(chunks):
                    h.mm(oacc[:, h8 * 64:(h8 + 1) * 64], VAe[:, vidx, h8, :], pt[:, c * 64:(c + 1) * 64],
                         c == 0, c == n - 1, [r_vae, ptr], [oaccr])

            def na_fin(i, oacc, oaccr):
                tq = i * 64
                rc, rcr = rcring.next()
                h.recip(rc[64:128, :], oacc[64:128, :], [oaccr], [rcr])
                ov = oacc[0:64, :].rearrange("p (c two d) -> p c two d", c=4, two=2)
                rv = rc[64:128, :].rearrange("p (c two d) -> p c two d", c=4, two=2)
                h.tt("dve", oa_sb[0:64, :, tq:tq + 64], ov[:, :, 0, :], rv[:, :, 0, :], ALU.mult, [oaccr, rcr], [r_oa])
                h.tt("dve", oa_sb[64:128, :, tq:tq + 64], ov[:, :, 1, :], rv[:, :, 1, :], ALU.mult, [oaccr, rcr], [r_oa])
            fin_pend = []
            for i in range(20):
                tq = i * 64
                if i < 16:
                    if i <= 3:
                        rel0, npair, ls, ei = -4, 6, i, i
                    elif i <= 12:
                        rel0, npair, ls, ei = -4, 4, i, None
                    else:
                        rel0, npair, ls, ei = -8, 6, i - 4, 4 + (i - 13)
                else:
                    rel0, npair, ls, ei = 0, 0, 0, None
                oacc, oaccr = oring.next()
                for h8 in range(8):
                    hp = (h8 % 2) * 64; ch = h8 // 2
                    q = QA_sb[hp:hp + 64, ch, tq:tq + 64]
                    s, sr = sring.next()
                    chunks = []
                    for j in range(npair):
                        hr = ls + 2 * j
                        ptype = (rel0 + 2 * j + 8) // 2
                        h.mm(s[:, j * 64:(j + 1) * 64], KA_sb[hp:hp + 64, ch, hr * 64:hr * 64 + 128], q, True, False,
                             [r_ka, r_qa], [sr])
                        h.mm(s[:, j * 64:(j + 1) * 64], identb, Tb[:, h8, ptype, :], False, True, [r_cb, r_tb], [sr])
                        chunks.append(hr // 2 if hr % 2 == 0 else 11 + (hr - 1) // 2)
                    for cj in range(2):
                        c = npair + cj
                        h.mm(s[:, c * 64:(c + 1) * 64], KA_sb[hp:hp + 64, ch, CT0 + cj * 128:CT0 + (cj + 1) * 128], q, True, True,
                             [r_ka, r_qa], [sr])
                        chunks.append(22 + cj)
                    pt, ptr = ptring.next()
                    if ei is None:
                        w = (npair + 2) * 64
                        h.act(pt[:, 0:w], s[:, 0:w], AF.Exp, [sr], [ptr], scale=0.125)
                    else:
                        for j in range(npair):
                            col = ei * 6 + j
                            h.act(pt[:, j * 64:(j + 1) * 64], s[:, j * 64:(j + 1) * 64], AF.Exp, [sr, r_rmk], [ptr],
                                  bias=RMK_sb[:, col:col + 1], scale=0.125)
                        h.act(pt[:, npair * 64:(npair + 2) * 64], s[:, npair * 64:(npair + 2) * 64], AF.Exp, [sr], [ptr], scale=0.125)
                    pend.append((i, h8, chunks, pt, ptr, oacc, oaccr))
                    if len(pend) > 1:
                        it = pend.pop(0)
                        na_pv(it)
                        if it[1] == 7:
                            na_fin(it[0], it[5], it[6])
            while pend:
                it = pend.pop(0)
                na_pv(it)
                if it[1] == 7:
                    na_fin(it[0], it[5], it[6])
            h.tt("pool", oa_sb, oa_sb, GA_sb, ALU.mult, [r_oa, r_ga], [r_oa])
            h.dma("sp", fm(OUTS[0:512, :]), oa_sb, [r_oa], [r_OUTS[0]])
            P.barrier()
            ar.off = mark0

        if "pool" in phases:
            BI_sb = ar.alloc([4, BIW], BF16); r_bi = R()
            GB_sb = ar.alloc([4, NT], BF16); r_gb = R()
            ob_sb = ar.alloc([4, NT], BF16); r_ob = R()
            icn = ar.alloc([4, NT]); r_icn = R()
            pwf = ar.alloc([4, 128]); pwb = ar.alloc([4, 128], BF16); r_pw = R()
            psc_sb = ar.alloc([4]); r_psc = R()
            Sa = ar.alloc([NL + 16]); Sb = ar.alloc([NL + 16]); r_sa = R(); r_sb_ = R()
            dd = ar.alloc([NT], BF16); r_dd = R()
            h.dma("sp", BI_sb, fm(BIh), [], [r_bi])
            h.dma("sp", GB_sb, fm(I["GB"]), [], [r_gb])
            for gi in range(4):
                h.dma("sp", icn[:, gi, :], bcast_rows(ICN, gi, 0, NT), [], [r_icn])
            h.dma("sp", pwf, pool_w.rearrange("(g m) n -> m g n", m=128), [], [r_pw])
            h.copy("dve", pwb, pwf, [r_pw], [r_pw])
            h.dma("sp", psc_sb, psc, [], [r_psc])
            pring = Ring(PS[0:4])
            for gi in range(4):
                for (so, L, tok0) in ((0, NL, 0), (NL + 16, NCX, NL)):
                    W = L + 16
                    u = BI_sb[:, gi, so:so + W]
                    h.tt("dve", Sa[:, 1:W], u[:, 0:W - 1], u[:, 1:W], ALU.add, [r_bi], [r_sa])
                    cur, rcur, oth, roth = Sa, r_sa, Sb, r_sb_
                    lo, hi = 1, W
                    for lev in range(gi):
                        sft = 1 << lev
                        nlo, nhi = lo + sft, hi - sft
                        h.tt("dve", oth[:, nlo:nhi], cur[:, nlo - sft:nhi - sft], cur[:, nlo + sft:nhi + sft], ALU.add, [rcur], [roth])
                        cur, rcur, oth, roth = oth, roth, cur, rcur
                        lo, hi = nlo, nhi
                    assert lo <= 8 and hi >= 8 + L, (lo, hi, L)
                    h.tt("dve", oth[:, 8:8 + L], cur[:, 8:8 + L], icn[:, gi, tok0:tok0 + L], ALU.mult, [rcur, r_icn], [roth])
                    h.tt("dve", dd[:, tok0:tok0 + L], oth[:, 8:8 + L], u[:, 8:8 + L], ALU.subtract, [roth, r_bi], [r_dd])
                for (t0, tn) in TCH:
                    pt, pr = pring.next()
                    h.mm(pt[:, 0:tn], pwb[:, gi, :], dd[:, t0:t0 + tn], True, True, [r_pw, r_dd], [pr])
                    h.stt("dve", ob_sb[:, gi, t0:t0 + tn], pt[:, 0:tn], psc_sb[:, gi:gi + 1], GB_sb[:, gi, t0:t0 + tn],
                          ALU.mult, ALU.mult, [pr, r_psc, r_gb], [r_ob])
            h.dma("sp", fm(OUTS[512:1024, :]), ob_sb, [r_ob], [r_OUTS[1]])
            P.barrier()
            ar.off = mark0

        if "conv" in phases:
            U_sb = ar.alloc([4, UW], BF16); r_u = R()
            GD_sb = ar.alloc([4, NT], BF16); r_gd = R()
            od_sb = ar.alloc([4, NT], BF16); r_od = R()
            acc = ar.alloc([4, NT]); r_acc = [R() for _ in range(4)]
            yb = ar.alloc([4, NT], BF16); sqb = ar.alloc([4, NT], BF16); r_yb = R(); r_sqb = R()
            zb = ar.alloc([4, NT], BF16); r_zb = R()
            cw_sb = ar.alloc([4, 31]); r_cw = R()
            cv_sb = ar.alloc([4, 3]); r_cv = R()
            pcf = ar.alloc([4, 512]); pcb = ar.alloc([4, 512], BF16); r_pc = R()
            mean_sb = ar.alloc([512]); r_mean = R()
            m2 = ar.alloc([512]); r_m2 = R()
            rstd = ar.alloc([512]); r_rstd = R()
            tring = Ring([ar.alloc([512]) for _ in range(3)])
            h.dma("sp", U_sb, fm(Uh), [], [r_u])
            h.dma("sp", GD_sb, fm(I["GD"]), [], [r_gd])
            h.dma("sp", cw_sb.rearrange("p a b -> p (a b)"), cw, [], [r_cw])
            h.dma("sp", cv_sb.rearrange("p a b -> p (a b)"), cvec, [], [r_cv])
            h.dma("sp", pcf, conv_pw.rearrange("(m p) n -> p m n", p=128), [], [r_pc])
            h.copy("dve", pcb, pcf, [r_pc], [r_pc])
            for chn in range(4):
                eng = "dve"
                for (so, L, tok0) in ((0, NL, 0), (NL + 30, NCX, NL)):
                    a = acc[:, chn, tok0:tok0 + L]
                    h.ts(eng, a, U_sb[:, chn, so:so + L], cw_sb[:, chn, 0:1], cv_sb[:, chn, 0:1], ALU.mult, ALU.add,
                         [r_u, r_cw, r_cv], [r_acc[chn]])
                    for k in range(1, 31):
                        h.stt(eng, a, U_sb[:, chn, so + k:so + k + L], cw_sb[:, chn, k:k + 1], a, ALU.mult, ALU.add,
                              [r_u, r_cw, r_acc[chn]], [r_acc[chn]])
                h.copy("act", yb[:, chn, :], acc[:, chn, :], [r_acc[chn]], [r_yb])
                h.act(sqb[:, chn, :], acc[:, chn, :], AF.Square, [r_acc[chn]], [r_sqb])
            pring = Ring(PS[0:6])
            for (t0, tn) in TCH:
                pm, pmr = pring.next()
                pe2, pe2r = pring.next()
                for chn in range(4):
                    h.mm(pm[:, 0:tn], onesb, yb[:, chn, t0:t0 + tn], chn == 0, chn == 3, [r_cb, r_yb], [pmr])
                for chn in range(4):
                    h.mm(pe2[:, 0:tn], onesb, sqb[:, chn, t0:t0 + tn], chn == 0, chn == 3, [r_cb, r_sqb], [pe2r])
                h.copy("act", mean_sb[:, 0:tn], pm[:, 0:tn], [pmr], [r_mean])
                h.tt("dve", m2[:, 0:tn], mean_sb[:, 0:tn], mean_sb[:, 0:tn], ALU.mult, [r_mean], [r_m2])
                h.tt("dve", m2[:, 0:tn], pe2[:, 0:tn], m2[:, 0:tn], ALU.subtract, [pe2r, r_m2], [r_m2])
                h.act(rstd[:, 0:tn], m2[:, 0:tn], AF.Sqrt, [r_m2, r_eps], [r_rstd], bias=eps_c[:, 0:1], scale=1.0)
                h.recip(rstd[:, 0:tn], rstd[:, 0:tn], [r_rstd], [r_rstd])
                for chn in range(4):
                    t_, tr_ = tring.next()
                    h.tt("dve", t_[:, 0:tn], acc[:, chn, t0:t0 + tn], mean_sb[:, 0:tn], ALU.subtract, [r_acc[chn], r_mean], [tr_])
                    h.tt("pool", t_[:, 0:tn], t_[:, 0:tn], rstd[:, 0:tn], ALU.mult, [tr_, r_rstd], [tr_])
                    h.act(zb[:, chn, t0:t0 + tn], t_[:, 0:tn], AF.Silu, [tr_, r_cv], [r_zb],
                          bias=cv_sb[:, chn, 2:3], scale=cv_sb[:, chn, 1:2])
                for nch in range(4):
                    pt, pr = pring.next()
                    for mch in range(4):
                        h.mm(pt[:, 0:tn], pcb[:, mch, nch * 128:(nch + 1) * 128], zb[:, mch, t0:t0 + tn], mch == 0, mch == 3,
                             [r_pc, r_zb], [pr])
                    h.tt("dve", od_sb[:, nch, t0:t0 + tn], pt[:, 0:tn], GD_sb[:, nch, t0:t0 + tn], ALU.mult, [pr, r_gd], [r_od])
            h.dma("sp", fm(OUTS[3 * 512:4 * 512, :]), od_sb, [r_od], [r_OUTS[3]])
            P.barrier()
            ar.off = mark0

        ym = ar.alloc([KC, NT], BF16); r_ym = R()
        mark2 = ar.off
        if "merge" in phases:
            hT = ar.alloc([KC, NT], BF16); r_hT = R()
            outs = ar.alloc([16, NT], BF16); r_outs = R()
            h.dma("sp", hT, HT.rearrange("p (a b) -> p a b", a=KC), [], [r_hT])
            for b in range(4):
                h.dma("sp", outs[:, b * 4:(b + 1) * 4, :], fm(OUTS[b * 512:(b + 1) * 512, :]), [r_OUTS[b]], [r_outs])
            wmring = Ring([ar.alloc([4, KC, 128], BF16) for _ in range(2)])
            wbring = Ring([ar.alloc([16, 128], BF16) for _ in range(2)])
            sgring = Ring([ar.alloc([512]) for _ in range(3)])
            acring = Ring([ar.alloc([512]) for _ in range(2)])
            tpring = Ring([ar.alloc([512]) for _ in range(2)])
            gring = Ring(PS[0:4])
            zring = Ring(PS[4:8])
            for dc in range(16):
                wm, wmr = wmring.next()
                wb, wbr = wbring.next()
                for b in range(4):
                    P.dma("pool", (lambda wm, b, dc: lambda e: e.dma_start(
                        out=wm[:, b, :, :], in_=w_m[:, b * D + dc * 128:b * D + (dc + 1) * 128].rearrange("(kc p) n -> p kc n", p=128)))(wm, b, dc),
                        writes=[wmr])
                P.dma("pool", (lambda wb, dc: lambda e: e.dma_start(
                    out=wb, in_=w_br[:, dc * 128:(dc + 1) * 128].rearrange("(j p) n -> p j n", p=128)))(wb, dc),
                    writes=[wbr])
                for (t0, tn) in TCH:
                    ac, acr = acring.next()
                    for b in range(4):
                        gp, gpr = gring.next()
                        zp, zpr = zring.next()
                        for kc in range(KC):
                            h.mm(gp[:, 0:tn], wm[:, b, kc, :], hT[:, kc, t0:t0 + tn], kc == 0, kc == KC - 1, [wmr, r_hT], [gpr])
                        for kq in range(4):
                            h.mm(zp[:, 0:tn], wb[:, b * 4 + kq, :], outs[:, b * 4 + kq, t0:t0 + tn], kq == 0, kq == 3, [wbr, r_outs], [zpr])
                        sg, sgr = sgring.next()
                        h.act(sg[:, 0:tn], gp[:, 0:tn], AF.Sigmoid, [gpr], [sgr])
                        if b == 0:
                            h.tt("dve", ac[:, 0:tn], sg[:, 0:tn], zp[:, 0:tn], ALU.mult, [sgr, zpr], [acr])
                        else:
                            tp, tpr = tpring.next()
                            h.tt("dve", tp[:, 0:tn], sg[:, 0:tn], zp[:, 0:tn], ALU.mult, [sgr, zpr], [tpr])
                            if b < 3:
                                h.tt("pool", ac[:, 0:tn], ac[:, 0:tn], tp[:, 0:tn], ALU.add, [acr, tpr], [acr])
                            else:
                                h.tt("pool", ym[:, dc, t0:t0 + tn], ac[:, 0:tn], tp[:, 0:tn], ALU.add, [acr, tpr], [r_ym])
            P.barrier()
            ar.off = mark2

        if "out" in phases:
            wo = ar.alloc([KC, D], BF16); r_wo = R()
            for kc in range(KC):
                P.dma("pool", (lambda kc: lambda e: e.dma_start(out=wo[:, kc, :], in_=w_out[kc * 128:(kc + 1) * 128, :]))(kc),
                      writes=[r_wo])
            G2 = [ar.alloc([D]) for _ in range(2)]; r_g2 = [R(), R()]
            gpo = ar.alloc([D]); r_gpo = R()
            h.dma("sp", gpo, bcast_rows(g_post, 0, 0, D), [], [r_gpo])
            for r in range(2):
                h.dma("sp", G2[r], bcast_rows(GATE, r, 0, D), [], [r_g2[r]])
                h.tt("dve", G2[r], G2[r], gpo, ALU.mult, [r_g2[r], r_gpo], [r_g2[r]])
            xring = Ring([ar.alloc([D]) for _ in range(2)])
            oring2 = Ring([ar.alloc([D]) for _ in range(2)])
            junk = ar.alloc([512], BF16); r_junk = R()
            stat = Ring([ar.alloc([8]) for _ in range(2)])
            t2ring = Ring([ar.alloc([512]) for _ in range(2)])
            bankset = [PS[0:4], PS[4:8]]
            bankr = [[R() for _ in range(4)], [R() for _ in range(4)]]
            for tt in range(NT // 128):
                r = 0 if tt < 8 else 1
                bs = bankset[tt % 2]; br = bankr[tt % 2]
                xt, xr = xring.next()
                h.dma("sp", xt, x[tt * 128:(tt + 1) * 128, :], [], [xr])
                st, sr = stat.next()
                for n in range(4):
                    for kc in range(KC):
                        h.mm(bs[n], ym[:, kc, tt * 128:(tt + 1) * 128], wo[:, kc, n * 512:(n + 1) * 512], kc == 0, kc == KC - 1,
                             [r_ym, r_wo], [br[n]])
                    h.act(junk, bs[n], AF.Square, [br[n]], [r_junk, sr], accum_out=st[:, n:n + 1])
                h.tt("dve", st[:, 4:5], st[:, 0:1], st[:, 1:2], ALU.add, [sr], [sr])
                h.tt("dve", st[:, 5:6], st[:, 2:3], st[:, 3:4], ALU.add, [sr], [sr])
                h.tt("dve", st[:, 4:5], st[:, 4:5], st[:, 5:6], ALU.add, [sr], [sr])
                h.act(st[:, 6:7], st[:, 4:5], AF.Sqrt, [sr, r_eps], [sr], bias=eps_c[:, 0:1], scale=1.0 / D)
                h.recip(st[:, 7:8], st[:, 6:7], [sr], [sr])
                ot, otr = oring2.next()
                for n in range(4):
                    t2, t2r = t2ring.next()
                    h.stt("dve", t2, bs[n], st[:, 7:8], G2[r][:, n * 512:(n + 1) * 512], ALU.mult, ALU.mult,
                          [br[n], sr, r_g2[r]], [t2r])
                    h.tt("pool", ot[:, n * 512:(n + 1) * 512], t2, xt[:, n * 512:(n + 1) * 512], ALU.add, [t2r, xr], [otr])
                h.dma("sp", XN[tt * 128:(tt + 1) * 128, :], ot, [otr], [])
        P.finish()
        P.build()
    return nc

bf16 = ml_dtypes.bfloat16
GRID_W = 64
def consts_for(q_norm_l, k_norm_l):
    ident = np.eye(128, dtype=np.float32)
    bd = np.zeros((128, 128), np.float32)
    bd[:64, :64] = 1.0 / 64; bd[64:, 64:] = 1.0 / 64
    rm = np.zeros((128, 128), np.float32)
    for m in range(128):
        if m % 32 < 16: rm[m + 16, m] = -1.0
        else: rm[m - 16, m] = 1.0
    qg = np.tile(q_norm_l, 2)[:, None].astype(np.float32)
    kg = np.tile(k_norm_l, 2)[:, None].astype(np.float32)
    return np.concatenate([ident, bd, rm, qg, kg], axis=1)
def cos_sin(core):
    t = 1024 * core + np.arange(1024)
    rows = (t // GRID_W).astype(np.float32); cols = (t % GRID_W).astype(np.float32)
    p = np.arange(128); d = p % 64; part = d // 32; j = d % 16
    freqs = (10000.0 ** (-(np.arange(16, dtype=np.float32)) / 16)).astype(np.float32)
    f = freqs[j]
    pos = np.where(part[:, None] == 0, rows[None, :], cols[None, :]).astype(np.float32)
    ang = (pos * f[:, None]).astype(np.float32)
    return np.concatenate([np.cos(ang), np.sin(ang)], axis=1).astype(np.float32)
def inputs_A(inp, l, core, ctx_l, x_c):
    c2 = np.stack([inp["c"][0], inp["c_ctx"]])
    c2T = np.ascontiguousarray(c2.reshape(2, 16, 128).transpose(2, 1, 0)).reshape(128, 32)
    return {
        "x": np.ascontiguousarray(np.concatenate([x_c, ctx_l], axis=0)),
        "c2T": c2T,
        "w_ada": inp["w_ada"][l], "b_ada2": np.ascontiguousarray(np.stack([inp["b_ada"][l]] * 2)),
        "g_pre": inp["g_pre"][l][None, :],
        "w_in": np.ascontiguousarray(inp["w_in"][l][:, :5888]),
        "consts": consts_for(inp["q_norm"][l], inp["k_norm"][l]),
        "cs": cos_sin(core),
    }

NEG = -30000.0
def to_bf(a): return np.ascontiguousarray(a).astype(bf16) if a.dtype != bf16 else np.ascontiguousarray(a)
def static_B():
    cq = np.arange(64); kc = np.arange(64)
    col_start = np.clip(cq - 8, 0, 48)
    col_ok = (kc[:, None] >= col_start[None, :]) & (kc[:, None] < col_start[None, :] + 16)
    cm = np.where(col_ok, 0.0, NEG * 8).astype(np.float32)
    cm128 = np.concatenate([cm, cm], axis=0)
    CMt = np.ascontiguousarray(np.tile(cm128[:, None, :], (1, 64, 1)).reshape(128, 4096))
    consts = np.concatenate([np.eye(128, dtype=np.float32), np.full((128, 128), 1.0 / 512, np.float32)], axis=1)
    return CMt, consts
def gather_T(rpb_l):
    cq = np.arange(64); kc = np.arange(64)
    co = np.clip(kc[:, None] - cq[None, :], -15, 15) + 15
    T = np.zeros((128, 8, 8, 64), np.float32)
    for h in range(8):
        for pt in range(8):
            rel_a = 2 * pt - 8
            for half in range(2):
                ro = rel_a + half + 7
                if 0 <= ro <= 14:
                    T[half * 64:(half + 1) * 64, h, pt, :] = rpb_l[h, ro][co]
    return np.ascontiguousarray(T.reshape(128, 4096))
def rowmask(core):
    M = np.zeros((128, 42), np.float32)
    for ei in range(7):
        i = ei if ei < 4 else 13 + (ei - 4)
        rel0 = -4 if ei < 4 else -8
        r = 16 * core + i
        blo = min(max(r - 4, 0), 120)
        for j in range(6):
            for half in range(2):
                g = r + rel0 + 2 * j + half
                ok = (0 <= g <= 127) and (blo <= g <= blo + 7)
                M[half * 64:(half + 1) * 64, ei * 6 + j] = 0.0 if ok else NEG
    return M
def invcnt(core):
    out = np.zeros((4, 1280), np.float32)
    for gi, k in enumerate((2, 4, 8, 16)):
        for (L, t, dst) in ((8192, 1024 * core + np.arange(1024), slice(0, 1024)), (256, np.arange(256), slice(1024, 1280))):
            lo = np.clip(t - k // 2, 0, L - 1); hi = np.clip(t + k - 1 - k // 2, 0, L - 1)
            out[gi, dst] = 1.0 / (hi - lo + 1).astype(np.float32)
    return out
def halo_cols(allarr, ctxarr, t_lo, t_hi, padc):
    F = allarr.shape[0]
    out = np.zeros((F, t_hi - t_lo), allarr.dtype)
    a, b = max(t_lo, 0), min(t_hi, allarr.shape[1])
    out[:, a - t_lo:b - t_lo] = allarr[:, a:b]
    if padc is None:
        return np.concatenate([out, ctxarr], axis=1)
    z = np.zeros((F, padc), allarr.dtype)
    return np.concatenate([out, z, ctxarr, z], axis=1)
def assemble_B(inp, l, Aout, xs, ctx_l, statics):
    CMt, constsB = statics
    cat = lambda k: np.concatenate([Aout[c][k][:, :1024] for c in range(8)], axis=1)
    catT = lambda k: np.concatenate([Aout[c][k][:1024] for c in range(8)], axis=0)
    KA_all = cat("KA"); BI_all = cat("BI"); U_all = cat("U"); KC_all = cat("KC")
    VA_all = catT("VA"); VC_all = catT("VC")
    A0 = Aout[0]
    KCall = np.ascontiguousarray(np.concatenate([KC_all, A0["KC"][:, 1024:]], axis=1))
    VCall = np.ascontiguousarray(np.concatenate([VC_all, A0["VC"][1024:]], axis=0))
    Tg = gather_T(inp["na_rpb"][l])
    cwl = inp["conv_w"][l]
    cw = np.ascontiguousarray(cwl.reshape(31, 4, 128).transpose(2, 1, 0)).reshape(128, 124)
    cv = np.stack([inp["conv_b"][l], inp["conv_ln_g"][l], inp["conv_ln_b"][l]], axis=1)
    cvec = np.ascontiguousarray(cv.reshape(4, 128, 3).transpose(1, 0, 2)).reshape(128, 12)
    common = {
        "g_post": inp["g_post"][l][None, :], "KCall": KCall, "VCall": VCall, "Tg": Tg, "CMt": CMt,
        "w_m": np.ascontiguousarray(inp["w_in"][l][:, 5888:]), "w_br": np.ascontiguousarray(inp["w_branch"][l].reshape(2048, 2048)),
        "w_out": inp["w_out"][l], "pool_w": np.ascontiguousarray(inp["pool_w"][l].reshape(512, 128)),
        "psc": np.ascontiguousarray(inp["pool_scale"][l].reshape(4, 128).T), "cw": cw, "cvec": cvec,
        "conv_pw": inp["conv_pw"][l], "consts": constsB,
    }
    ims = []
    for c in range(8):
        A = Aout[c]
        r0 = 16 * c
        d = dict(common)
        d["x"] = np.ascontiguousarray(np.concatenate([xs[c], ctx_l], axis=0))
        d["GATE"] = np.ascontiguousarray(A["MOD"][:, 4096:6144])
        d["HT"] = A["HT"]
        for k in ("QA", "QC", "GA", "GB", "GC", "GD"):
            d[k] = A[k]
        d["KAh"] = np.ascontiguousarray(halo_cols(KA_all, A0["KA"][:, 1024:], (r0 - 4) * 64, (r0 + 19) * 64, None))
        vah = halo_cols(np.ascontiguousarray(VA_all.T), np.ascontiguousarray(A0["VA"][1024:].T), (r0 - 4) * 64, (r0 + 19) * 64, None)
        d["VAh"] = np.ascontiguousarray(vah.T)
        d["BIh"] = np.ascontiguousarray(halo_cols(BI_all, A0["BI"][:, 1024:], 1024 * c - 8, 1024 * c + 1024 + 8, 8))
        d["Uh"] = np.ascontiguousarray(halo_cols(U_all, A0["U"][:, 1024:], 1024 * c - 15, 1024 * c + 1024 + 15, 15))
        d["RMK"] = rowmask(c)
        d["ICN"] = invcnt(c)
        ims.append(d)
    return ims

_CACHE = {}


def kernel(**inputs):
    inp = {k: np.asarray(v) for k, v in inputs.items()}
    if "A" not in _CACHE:
        _CACHE["A"] = build_A()
        _CACHE["B"] = build_B()
        _CACHE["S"] = static_B()
    ncA, ncB, statics = _CACHE["A"], _CACHE["B"], _CACHE["S"]
    cores = list(range(8))
    xs = [np.ascontiguousarray(inp["x"][0, 1024 * c:1024 * (c + 1)]) for c in cores]
    ctx_l = np.ascontiguousarray(inp["ctx"][0])
    for l in range(4):
        imsA = [inputs_A(inp, l, c, ctx_l, xs[c]) for c in cores]
        resA = run_bass_kernel_spmd(ncA, imsA, core_ids=cores)
        Aout = [{k: np.asarray(v) for k, v in resA.results[c].items()} for c in cores]
        imsB = assemble_B(inp, l, Aout, xs, ctx_l, statics)
        resB = run_bass_kernel_spmd(ncB, imsB, core_ids=cores)
        XN = [np.asarray(resB.results[c]["XN"]) for c in cores]
        xs = [np.ascontiguousarray(XN[c][:1024]) for c in cores]
        ctx_l = np.ascontiguousarray(XN[0][1024:])
    return np.concatenate(xs, axis=0)[None].astype(np.float32)
```

```python
import contextlib
import numpy as np
import ml_dtypes
import concourse.bass as bass
import concourse.mybir as mybir
from concourse.bass_utils import run_bass_kernel_spmd

F32 = mybir.dt.float32
BF16 = mybir.dt.bfloat16
AF = mybir.ActivationFunctionType
ALU = mybir.AluOpType

D = 2048
KC = 16
NL = 1024
NCX = 256
NT = 1280
SMALL = 5888
TCH = [(0, 512), (512, 512), (1024, 256)]
EPS = 1e-6
NEG = -30000.0
SHIFT_NA = 0.0
SHIFT_GQA = 0.0


class R:
    __slots__ = ("w", "r")

    def __init__(self):
        self.w = None
        self.r = {}


class Prog:
    CE = ("pe", "act", "dve", "pool")
    QS = ("sp", "act", "pool")
    NRING = 12

    def __init__(self, nc):
        self.nc = nc
        self.streams = {e: [] for e in ("pe", "act", "dve", "pool", "sp")}
        self.cnt = {e: 0 for e in self.CE}
        self.known = {e: {} for e in self.streams}
        self.dma_i = {q: 0 for q in self.QS}
        self.sems = {}
        self.semnames = list(self.CE) + [f"d{q}{i}" for q in self.QS for i in range(self.NRING)]

    def _collect(self, reads, writes):
        need = {}
        for r in reads:
            if r.w is not None and need.get(r.w[0], 0) < r.w[1]:
                need[r.w[0]] = r.w[1]
        for w in writes:
            if w.w is not None and need.get(w.w[0], 0) < w.w[1]:
                need[w.w[0]] = w.w[1]
            for k, v in w.r.items():
                if need.get(k, 0) < v:
                    need[k] = v
        return need

    def _emit_waits(self, eng, need, skip_key=None):
        kn = self.known[eng]
        for k, v in need.items():
            if k == skip_key:
                continue
            if kn.get(k, 0) < v:
                kn[k] = v
                self.streams[eng].append(("wait", k, v))

    def op(self, eng, fn, reads=(), writes=(), nosync_self=False):
        need = self._collect(reads, writes)
        self._emit_waits(eng, need, skip_key=eng if nosync_self else None)
        self.cnt[eng] += 1
        n = self.cnt[eng]
        self.streams[eng].append(("op", fn, eng, 1))
        for r in reads:
            r.r[eng] = n
        for w in writes:
            w.w = (eng, n)
            w.r = {}
        return n

    def prewait(self, eng, reads=(), writes=()):
        need = self._collect(reads, writes)
        self._emit_waits(eng, need, skip_key=eng)

    def dma(self, q, fn, reads=(), writes=()):
        need = self._collect(reads, writes)
        i = self.dma_i[q]
        self.dma_i[q] += 1
        key = f"d{q}{i % self.NRING}"
        prev = 16 * (i // self.NRING)
        if prev > 0 and need.get(key, 0) < prev:
            need[key] = prev
        self._emit_waits(q, need)
        val = prev + 16
        self.streams[q].append(("op", fn, key, 16))
        for r in reads:
            r.r[key] = val
        for w in writes:
            w.w = (key, val)
            w.r = {}

    def _all_done(self):
        need = {ce: self.cnt[ce] for ce in self.CE if self.cnt[ce] > 0}
        for q in self.QS:
            n = self.dma_i[q]
            for i in range(max(0, n - self.NRING), n):
                need[f"d{q}{i % self.NRING}"] = 16 * (i // self.NRING + 1)
        return need

    def barrier(self):
        need = self._all_done()
        for eng in self.streams:
            self._emit_waits(eng, need)

    def finish(self):
        self._emit_waits("sp", self._all_done())

    def build(self):
        nc = self.nc
        with contextlib.ExitStack() as es:
            for name in self.semnames:
                self.sems[name] = es.enter_context(nc.semaphore(name))
            block = es.enter_context(nc.Block())
            sems = self.sems

            def run(stream):
                def f(eng):
                    for it in stream:
                        if it[0] == "wait":
                            eng.wait_ge(sems[it[1]], it[2])
                        else:
                            it[1](eng).then_inc(sems[it[2]], it[3])
                return f
            block.tensor(run(self.streams["pe"]))
            block.scalar(run(self.streams["act"]))
            block.vector(run(self.streams["dve"]))
            block.gpsimd(run(self.streams["pool"]))
            block.sync(run(self.streams["sp"]))


class Arena:
    def __init__(self, tensor, words):
        self.t = tensor
        self.cap = words
        self.off = 0

    def reset(self):
        self.off = 0

    def alloc(self, free_shape, dt=F32, parts=128):
        n = int(np.prod(free_shape))
        words = n if dt == F32 else (n + 1) // 2
        words = (words + 7) // 8 * 8
        start = self.off
        self.off += words
        assert self.off <= self.cap, f"arena overflow {self.off} > {self.cap}"
        ap = self.t[0:parts, start:start + (n if dt == F32 else (n + 1) // 2)]
        if dt != F32:
            ap = ap.bitcast(dt)[:, 0:n]
        if len(free_shape) == 2:
            ap = ap.rearrange("p (a b) -> p a b", a=free_shape[0])
        elif len(free_shape) == 3:
            ap = ap.rearrange("p (a b c) -> p a b c", a=free_shape[0], b=free_shape[1])
        return ap


class Ring:
    def __init__(self, tiles):
        self.tiles = tiles
        self.rs = [R() for _ in tiles]
        self.i = 0

    def next(self):
        i = self.i % len(self.tiles)
        self.i += 1
        return self.tiles[i], self.rs[i]


class H:
    def __init__(self, P):
        self.P = P
        self.flip = 0

    def mm(self, out, lhsT, rhs, start, stop, reads, writes):
        self.P.op("pe", lambda e: e.matmul(out, lhsT=lhsT, rhs=rhs, start=start, stop=stop),
                  reads=reads, writes=writes, nosync_self=True)

    def tr(self, out, in_, ident, reads, writes):
        self.P.op("pe", lambda e: e.transpose(out=out, in_=in_, identity=ident),
                  reads=reads, writes=writes, nosync_self=True)

    def act(self, out, in_, func, reads, writes, bias=None, scale=None, accum_out=None):
        kw = {}
        if bias is not None:
            kw["bias"] = bias
        if scale is not None:
            kw["scale"] = scale
        if accum_out is not None:
            kw["accum_out"] = accum_out
        self.P.op("act", lambda e: e.activation(out=out, in_=in_, func=func, **kw), reads=reads, writes=writes)

    def copy(self, eng, out, in_, reads, writes):
        if eng == "act":
            self.P.op("act", lambda e: e.copy(out=out, in_=in_), reads=reads, writes=writes)
        else:
            self.P.op(eng, lambda e: e.tensor_copy(out=out, in_=in_), reads=reads, writes=writes)

    def anycopy(self, out, in_, reads, writes):
        self.flip ^= 1
        self.copy("act" if self.flip else "dve", out, in_, reads, writes)

    def tt(self, eng, out, in0, in1, op, reads, writes):
        self.P.op(eng, lambda e: e.tensor_tensor(out=out, in0=in0, in1=in1, op=op), reads=reads, writes=writes)

    def ts(self, eng, out, in0, s1, s2, op0, op1, reads, writes):
        if s2 is None:
            self.P.op(eng, lambda e: e.tensor_scalar(out=out, in0=in0, scalar1=s1, scalar2=None, op0=op0),
                      reads=reads, writes=writes)
        else:
            self.P.op(eng, lambda e: e.tensor_scalar(out=out, in0=in0, scalar1=s1, scalar2=s2, op0=op0, op1=op1),
                      reads=reads, writes=writes)

    def stt(self, eng, out, in0, scalar, in1, op0, op1, reads, writes):
        self.P.op(eng, lambda e: e.scalar_tensor_tensor(out=out, in0=in0, scalar=scalar, in1=in1, op0=op0, op1=op1),
                  reads=reads, writes=writes)

    def recip(self, out, in_, reads, writes):
        self.P.op("dve", lambda e: e.reciprocal(out=out, in_=in_), reads=reads, writes=writes)

    def memset(self, eng, out, val, writes):
        self.P.op(eng, lambda e: e.memset(out, val), writes=writes)

    def dma(self, q, out, in_, reads, writes):
        self.P.dma(q, lambda e: e.dma_start(out=out, in_=in_), reads=reads, writes=writes)


def bcast_rows(ap2d, row, c0, n, parts=128):
    ncols = ap2d.shape[1]
    return bass.AP(ap2d.tensor, ap2d.offset + row * ncols + c0, [[0, parts], [1, n]])


def fm(ap2d):
    return ap2d.rearrange("(ch p) t -> p ch t", p=128)


BLK_TYPES = {}
for b in range(0, 4): BLK_TYPES[b] = ("copy", "KA", b)
for b in range(4, 8): BLK_TYPES[b] = ("tm", "VA", b - 4)
BLK_TYPES[8] = ("qk", "KC", 0)
BLK_TYPES[9] = ("tm", "VC", 0)
for b in range(10, 14): BLK_TYPES[b] = ("copy", "QA", b - 10)
for b in range(14, 18): BLK_TYPES[b] = ("qk", "QC", b - 14)
for b in range(18, 22): BLK_TYPES[b] = ("silu", "GA", b - 18)
for b in range(22, 26): BLK_TYPES[b] = ("copy", "BI", b - 22)
for b in range(26, 30): BLK_TYPES[b] = ("silu", "GB", b - 26)
for b in range(30, 34): BLK_TYPES[b] = ("silu", "GC", b - 30)
for b in range(34, 38): BLK_TYPES[b] = ("glu_a", "U", b - 34)
for b in range(38, 42): BLK_TYPES[b] = ("glu_g", "U", b - 38)
for b in range(42, 46): BLK_TYPES[b] = ("silu", "GD", b - 42)


def build_A():
    nc = bass.Bass("TRN2", target_bir_lowering=False)
    P = Prog(nc)
    h = H(P)

    def din(n, s, dt=F32):
        return nc.dram_tensor(n, s, dt, kind="ExternalInput").ap()

    def dout(n, s, dt=BF16):
        return nc.dram_tensor(n, s, dt, kind="ExternalOutput").ap()
    x = din("x", [NT, D])
    c2T = din("c2T", [128, KC * 2])
    w_ada = din("w_ada", [D, 3 * D])
    b_ada2 = din("b_ada2", [2, 3 * D])
    g_pre = din("g_pre", [1, D])
    w_in = din("w_in", [D, SMALL])
    consts = din("consts", [128, 386])
    cs = din("cs", [128, 2 * NL])
    MOD = dout("MOD", [2, 3 * D], F32)
    HT = dout("HT", [128, KC * NT])
    O = {n: dout(n, [512, NT]) for n in ("KA", "QA", "QC", "GA", "GB", "GC", "GD", "BI", "U")}
    O["KC"] = dout("KC", [128, NT])
    O["VA"] = dout("VA", [NT, 512])
    O["VC"] = dout("VC", [NT, 128])

    with contextlib.ExitStack() as es:
        arena_t = es.enter_context(nc.sbuf_tensor("arena", [128, 51200], F32))
        ar = Arena(arena_t, 51200)
        pst = [es.enter_context(nc.psum_tensor(f"ps{i}", [128, 512], F32)) for i in range(8)]
        psr = Ring([t[:] for t in pst])

        cst = ar.alloc([386]); r_cst = R()
        identb = ar.alloc([128], BF16); bdb = ar.alloc([128], BF16); rmb = ar.alloc([128], BF16); r_cb = R()
        hT = ar.alloc([KC, NT], BF16); r_hT = R()
        wring = Ring([ar.alloc([KC, 512], BF16) for _ in range(3)])
        eps_c = ar.alloc([1]); r_eps = R()
        mark = ar.off
        h.memset("dve", eps_c, EPS, [r_eps])

        h.dma("sp", cst, consts, [], [r_cst])
        h.copy("dve", identb, cst[:, 0:128], [r_cst], [r_cb])
        h.copy("dve", bdb, cst[:, 128:256], [r_cst], [r_cb])
        h.copy("dve", rmb, cst[:, 256:384], [r_cst], [r_cb])

        scf = ar.alloc([KC * 2]); scb = ar.alloc([KC, 2], BF16); r_sc = R()
        h.dma("sp", scf, c2T, [], [r_sc])
        h.act(scb.rearrange("p a b -> p (a b)"), scf, AF.Silu, [r_sc], [r_sc])
        b2ring = Ring([ar.alloc([512], F32, parts=2) for _ in range(2)])
        mring = Ring([ar.alloc([512], F32, parts=2) for _ in range(2)])
        r_MOD = R()
        for n in range(12):
            wt, wr = wring.next()
            P.dma("pool", (lambda wt, n: lambda e: e.dma_start(
                out=wt, in_=w_ada[:, n * 512:(n + 1) * 512].rearrange("(kc p) n -> p kc n", p=128)))(wt, n),
                writes=[wr])
            bt, br = b2ring.next()
            h.dma("sp", bt, b_ada2[:, n * 512:(n + 1) * 512], [], [br])
            pt, pr = psr.next()
            for kc in range(KC):
                h.mm(pt[0:2, :], scb[:, kc, :], wt[:, kc, :], kc == 0, kc == KC - 1, [r_sc, wr], [pr])
            mt, mr = mring.next()
            h.tt("dve", mt, pt[0:2, :], bt, ALU.add, [pr, br], [mr])
            h.dma("sp", MOD[:, n * 512:(n + 1) * 512], mt, [mr], [r_MOD])
        A1 = [ar.alloc([D]) for _ in range(2)]
        SH = [ar.alloc([D]) for _ in range(2)]
        r_A1 = [R(), R()]; r_SH = [R(), R()]
        gpb = ar.alloc([D]); r_gpb = R()
        h.dma("sp", gpb, bcast_rows(g_pre, 0, 0, D), [], [r_gpb])
        for r in range(2):
            h.dma("sp", SH[r], bcast_rows(MOD, r, 0, D), [r_MOD], [r_SH[r]])
            h.dma("sp", A1[r], bcast_rows(MOD, r, D, D), [r_MOD], [r_A1[r]])
            h.stt("dve", A1[r], A1[r], 1.0, gpb, ALU.add, ALU.mult, [r_A1[r], r_gpb], [r_A1[r]])
        xring = Ring([ar.alloc([D]) for _ in range(2)])
        tring = Ring([ar.alloc([D]) for _ in range(1)])
        hbring = Ring([ar.alloc([D], BF16) for _ in range(2)])
        junk = ar.alloc([D], BF16); r_junk = R()
        stat = Ring([ar.alloc([4]) for _ in range(2)])
        for tt in range(NT // 128):
            r = 0 if tt < 8 else 1
            xt, xr = xring.next()
            h.dma("sp", xt, x[tt * 128:(tt + 1) * 128, :], [], [xr])
            st, sr = stat.next()
            h.act(junk, xt, AF.Square, [xr], [r_junk, sr], accum_out=st[:, 0:1])
            h.act(st[:, 1:2], st[:, 0:1], AF.Sqrt, [sr, r_eps], [sr], bias=eps_c[:, 0:1], scale=1.0 / D)
            h.recip(st[:, 2:3], st[:, 1:2], [sr], [sr])
            t_, tr_ = tring.next()
            h.stt("dve", t_, xt, st[:, 2:3], A1[r], ALU.mult, ALU.mult, [xr, sr, r_A1[r]], [tr_])
            hb, hbr = hbring.next()
            h.tt("pool", hb, t_, SH[r], ALU.add, [tr_, r_SH[r]], [hbr])
            for j in range(4):
                pt, pr = psr.next()
                ptb = pt.bitcast(BF16)
                for q in range(4):
                    kc = j * 4 + q
                    h.tr(ptb[:, q * 128:(q + 1) * 128], hb[:, kc * 128:(kc + 1) * 128], identb, [hbr, r_cb], [pr])
                h.anycopy(hT[:, j * 4:(j + 1) * 4, tt * 128:(tt + 1) * 128],
                          ptb[:, 0:512].rearrange("p (a b) -> p a b", a=4), [pr], [r_hT])
        h.dma("sp", HT.rearrange("p (a b) -> p a b", a=KC), hT, [r_hT], [])
        P.barrier()
        ar.off = mark

        cs_sb = ar.alloc([2 * NL]); r_cs = R()
        h.dma("sp", cs_sb, cs, [], [r_cs])
        a_sb = ar.alloc([4, NT]); r_a = [R() for _ in range(4)]
        stg = Ring([ar.alloc([512], BF16) for _ in range(6)])
        tmp = Ring([ar.alloc([512]) for _ in range(8)])

        def epilogue(kind, dst, j, pt, pr, t0, tn):
            if kind in ("copy", "silu"):
                s, sr = stg.next()
                if kind == "copy":
                    h.anycopy(s[:, 0:tn], pt[:, 0:tn], [pr], [sr])
                else:
                    h.act(s[:, 0:tn], pt[:, 0:tn], AF.Silu, [pr], [sr])
                h.dma("sp", O[dst][j * 128:(j + 1) * 128, t0:t0 + tn], s[:, 0:tn], [sr], [])
            elif kind == "glu_a":
                h.copy("dve", a_sb[:, j, t0:t0 + tn], pt[:, 0:tn], [pr], [r_a[j]])
            elif kind == "glu_g":
                g, gr = tmp.next()
                h.act(g[:, 0:tn], pt[:, 0:tn], AF.Sigmoid, [pr], [gr])
                s, sr = stg.next()
                h.tt("dve", s[:, 0:tn], a_sb[:, j, t0:t0 + tn], g[:, 0:tn], ALU.mult, [gr, r_a[j]], [sr])
                h.dma("sp", O[dst][j * 128:(j + 1) * 128, t0:t0 + tn], s[:, 0:tn], [sr], [])
            elif kind == "qk":
                gcol = cst[:, 384:385] if dst == "QC" else cst[:, 385:386]
                sq, sqr = stg.next()
                h.act(sq[:, 0:tn], pt[:, 0:tn], AF.Square, [pr], [sqr])
                p2, p2r = psr.next()
                h.mm(p2[:, 0:tn], bdb, sq[:, 0:tn], True, True, [sqr, r_cb], [p2r])
                rs, rsr = tmp.next()
                h.act(rs[:, 0:tn], p2[:, 0:tn], AF.Sqrt, [p2r, r_eps], [rsr], bias=eps_c[:, 0:1], scale=1.0)
                h.recip(rs[:, 0:tn], rs[:, 0:tn], [rsr], [rsr])
                qn, qnr = tmp.next()
                h.stt("dve", qn[:, 0:tn], pt[:, 0:tn], gcol, rs[:, 0:tn], ALU.mult, ALU.mult, [pr, rsr, r_cst], [qnr])
                s, sr = stg.next()
                if t0 < NL:
                    qb, qbr = stg.next()
                    h.copy("act", qb[:, 0:tn], qn[:, 0:tn], [qnr], [qbr])
                    p3, p3r = psr.next()
                    h.mm(p3[:, 0:tn], rmb, qb[:, 0:tn], True, True, [qbr, r_cb], [p3r])
                    t1, t1r = tmp.next()
                    h.tt("dve", t1[:, 0:tn], qn[:, 0:tn], cs_sb[:, t0:t0 + tn], ALU.mult, [qnr, r_cs], [t1r])
                    t2, t2r = tmp.next()
                    h.tt("dve", t2[:, 0:tn], p3[:, 0:tn], cs_sb[:, NL + t0:NL + t0 + tn], ALU.mult, [p3r, r_cs], [t2r])
                    h.tt("pool", s[:, 0:tn], t1[:, 0:tn], t2[:, 0:tn], ALU.add, [t1r, t2r], [sr])
                else:
                    h.copy("act", s[:, 0:tn], qn[:, 0:tn], [qnr], [sr])
                h.dma("sp", O[dst][j * 128:(j + 1) * 128, t0:t0 + tn], s[:, 0:tn], [sr], [])

        ngroups = (SMALL + 511) // 512
        for g in range(ngroups):
            c0 = g * 512
            wcols = min(512, SMALL - c0)
            wt, wr = wring.next()
            P.dma("pool", (lambda wt, c0, wcols: lambda e: e.dma_start(
                out=wt[:, :, 0:wcols], in_=w_in[:, c0:c0 + wcols].rearrange("(kc p) n -> p kc n", p=128)))(wt, c0, wcols),
                writes=[wr])
            b = g * 4
            nb = wcols // 128
            bi = 0
            while bi < nb:
                kind, dst, j = BLK_TYPES[b + bi]
                if kind == "tm":
                    run = 1
                    while bi + run < nb and BLK_TYPES[b + bi + run][0] == "tm" and BLK_TYPES[b + bi + run][1] == dst:
                        run += 1
                    ncol = run * 128
                    co = bi * 128
                    for tt in range(NT // 128):
                        pt, pr = psr.next()
                        for kc in range(KC):
                            h.mm(pt[:, 0:ncol], hT[:, kc, tt * 128:(tt + 1) * 128], wt[:, kc, co:co + ncol],
                                 kc == 0, kc == KC - 1, [r_hT, wr], [pr])
                        s, sr = stg.next()
                        h.anycopy(s[:, 0:ncol], pt[:, 0:ncol], [pr], [sr])
                        h.dma("sp", O[dst][tt * 128:(tt + 1) * 128, j * 128:j * 128 + ncol], s[:, 0:ncol], [sr], [])
                    bi += run
                else:
                    co = bi * 128
                    for (t0, tn) in TCH:
                        pt, pr = psr.next()
                        for kc in range(KC):
                            h.mm(pt[:, 0:tn], wt[:, kc, co:co + 128], hT[:, kc, t0:t0 + tn],
                                 kc == 0, kc == KC - 1, [r_hT, wr], [pr])
                        epilogue(kind, dst, j, pt, pr, t0, tn)
                    bi += 1
        P.finish()
        P.build()
    return nc


NHALO = 23
KAW = NHALO * 64 + NCX
BIW = (NL + 16) + (NCX + 16)
UW = (NL + 30) + (NCX + 30)
NKC = (8 * NL + NCX) // 128


def build_B(phases=("gqa", "na", "pool", "conv", "merge", "out")):
    nc = bass.Bass("TRN2", target_bir_lowering=False)
    P = Prog(nc)
    h = H(P)

    def din(n, s, dt=F32):
        return nc.dram_tensor(n, s, dt, kind="ExternalInput").ap()
    x = din("x", [NT, D])
    GATE = din("GATE", [2, D])
    g_post = din("g_post", [1, D])
    HT = din("HT", [128, KC * NT], BF16)
    I = {n: din(n, [512, NT], BF16) for n in ("QA", "QC", "GA", "GB", "GC", "GD")}
    KCall = din("KCall", [128, NKC * 128], BF16)
    VCall = din("VCall", [NKC * 128, 128], BF16)
    KAh = din("KAh", [512, KAW], BF16)
    VAh = din("VAh", [KAW, 512], BF16)
    BIh = din("BIh", [512, BIW], BF16)
    Uh = din("Uh", [512, UW], BF16)
    Tg = din("Tg", [128, 4096])
    CMt = din("CMt", [128, 4096])
    RMK = din("RMK", [128, 42])
    ICN = din("ICN", [4, NT])
    w_m = din("w_m", [D, 4 * D])
    w_br = din("w_br", [4 * 512, D])
    w_out = din("w_out", [D, D])
    pool_w = din("pool_w", [512, 128])
    psc = din("psc", [128, 4])
    cw = din("cw", [128, 4 * 31])
    cvec = din("cvec", [128, 12])
    conv_pw = din("conv_pw", [512, 512])
    consts = din("consts", [128, 256])
    XN = nc.dram_tensor("XN", [NT, D], F32, kind="ExternalOutput").ap()
    OUTS = nc.dram_tensor("OUTS", [4 * 512, NT], BF16).ap()
    r_OUTS = [R() for _ in range(4)]

    with contextlib.ExitStack() as es:
        arena_t = es.enter_context(nc.sbuf_tensor("arena", [128, 51200], F32))
        ar = Arena(arena_t, 51200)
        pst = [es.enter_context(nc.psum_tensor(f"ps{i}", [128, 512], F32)) for i in range(8)]
        PS = [t[:] for t in pst]

        cst = ar.alloc([256]); r_cst = R()
        identb = ar.alloc([128], BF16); onesb = ar.alloc([128], BF16); r_cb = R()
        eps_c = ar.alloc([1]); r_eps = R()
        h.memset("dve", eps_c, EPS, [r_eps])
        h.dma("sp", cst, consts, [], [r_cst])
        h.copy("dve", identb, cst[:, 0:128], [r_cst], [r_cb])
        h.copy("dve", onesb, cst[:, 128:256], [r_cst], [r_cb])
        mark0 = ar.off

        if "gqa" in phases:
            QC_sb = ar.alloc([4, NT], BF16); r_qc = R()
            GC_sb = ar.alloc([4, NT], BF16); r_gc = R()
            oc_sb = ar.alloc([4, NT], BF16); r_oc = R()
            KC2 = ar.alloc([2, NKC * 128], BF16); r_kc2 = R()
            VCe = ar.alloc([NKC, 2, 128], BF16); r_vce = R()
            VCt = ar.alloc([NKC, 128], BF16); r_vct = R()
            h.dma("sp", QC_sb, fm(I["QC"]), [], [r_qc])
            h.dma("sp", GC_sb, fm(I["GC"]), [], [r_gc])
            for g in range(2):
                for hp in (0, 64):
                    h.dma("sp", KC2[hp:hp + 64, g, :], KCall[g * 64:(g + 1) * 64, :], [], [r_kc2])
            vsrc = VCall.rearrange("(kc p) c -> p kc c", p=128)
            for a in range(0, NKC, 11):
                h.dma("sp", VCt[:, a:a + 11, :], vsrc[:, a:a + 11, :], [], [r_vct])
            h.memset("pool", VCe.rearrange("p a b c -> p (a b c)"), 1.0, [r_vce])
            for g in range(2):
                h.copy("pool", VCe[:, :, g, 0:64], VCt[:, :, g * 64:(g + 1) * 64], [r_vct], [r_vce])
            ptring = Ring([ar.alloc([512], BF16) for _ in range(8)])
            rcring = Ring([ar.alloc([512]) for _ in range(2)])
            sring = Ring(PS[0:4])
            accs = PS[4:8]
            acc_r = [R() for _ in range(4)]

            def gqa_fin(h8, acc, accr, t0, tn):
                hp = (h8 % 2) * 64; ch = h8 // 2
                rc, rcr = rcring.next()
                h.recip(rc[64:128, 0:tn], acc[64:128, 0:tn], [accr], [rcr])
                h.tt("dve", oc_sb[hp:hp + 64, ch, t0:t0 + tn], acc[0:64, 0:tn], rc[64:128, 0:tn], ALU.mult,
                     [accr, rcr], [r_oc])
            for g in range(2):
                heads = [4 * g, 4 * g + 2, 4 * g + 1, 4 * g + 3]
                for qh in range(2):
                    t0 = qh * 512; tn = 512
                    pend = []

                    def do_pv(item, g=g):
                        idx, kc, grp = item
                        P.prewait("pe", reads=[x[2] for x in grp], writes=acc_r)
                        for (a_, pt, ptr) in grp:
                            h.mm(accs[a_][:, 0:512], VCe[:, kc, g, :], pt[:, 0:512], idx == 0, idx == NKC - 1, [r_vce, ptr], [acc_r[a_]])
                    for idx in range(NKC):
                        kc = idx
                        grp = []
                        P.prewait("pe", writes=sring.rs)
                        for a_, h8 in enumerate(heads):
                            hp = (h8 % 2) * 64; ch = h8 // 2
                            s_, sr = sring.next()
                            h.mm(s_[:, 0:512], KC2[hp:hp + 64, g, kc * 128:(kc + 1) * 128], QC_sb[hp:hp + 64, ch, t0:t0 + 512],
                                 True, True, [r_kc2, r_qc], [sr])
                            pt, ptr = ptring.next()
                            h.act(pt[:, 0:512], s_[:, 0:512], AF.Exp, [sr], [ptr], scale=0.125)
                            grp.append((a_, pt, ptr))
                        pend.append((idx, kc, grp))
                        if len(pend) > 1:
                            do_pv(pend.pop(0))
                    while pend:
                        do_pv(pend.pop(0))
                    for a_, h8 in enumerate(heads):
                        gqa_fin(h8, accs[a_], acc_r[a_], t0, tn)
            for h8 in range(8):
                g = h8 // 4; hp = (h8 % 2) * 64; ch = h8 // 2
                t0 = NL; tn = NCX
                acc = accs[h8 % 4]; accr = acc_r[h8 % 4]
                pts = []
                for kc in (NKC - 2, NKC - 1):
                    s_, sr = sring.next()
                    h.mm(s_[:, 0:tn], KC2[hp:hp + 64, g, kc * 128:(kc + 1) * 128], QC_sb[hp:hp + 64, ch, t0:t0 + tn],
                         True, True, [r_kc2, r_qc], [sr])
                    pt, ptr = ptring.next()
                    h.act(pt[:, 0:tn], s_[:, 0:tn], AF.Exp, [sr], [ptr], scale=0.125)
                    pts.append((kc, pt, ptr))
                for j, (kc, pt, ptr) in enumerate(pts):
                    h.mm(acc[:, 0:tn], VCe[:, kc, g, :], pt[:, 0:tn], j == 0, j == 1, [r_vce, ptr], [accr])
                gqa_fin(h8, acc, accr, t0, tn)
            h.tt("pool", oc_sb, oc_sb, GC_sb, ALU.mult, [r_oc, r_gc], [r_oc])
            h.dma("sp", fm(OUTS[2 * 512:3 * 512, :]), oc_sb, [r_oc], [r_OUTS[2]])
            P.barrier()
            ar.off = mark0

        if "na" in phases:
            QA_sb = ar.alloc([4, NT], BF16); r_qa = R()
            GA_sb = ar.alloc([4, NT], BF16); r_ga = R()
            oa_sb = ar.alloc([4, NT], BF16); r_oa = R()
            KA_sb = ar.alloc([4, KAW], BF16); r_ka = R()
            VAe = ar.alloc([24, 8, 128], BF16); r_vae = R()
            Tb = ar.alloc([8, 8, 64], BF16); r_tb = R()
            RMK_sb = ar.alloc([42]); r_rmk = R()
            ptring = Ring([ar.alloc([512], BF16) for _ in range(3)])
            rcring = Ring([ar.alloc([512]) for _ in range(2)])
            mark1 = ar.off
            VAt = ar.alloc([11, 512], BF16); r_vat = R()
            Tg_sb = ar.alloc([4096]); r_tg = R()
            CM_sb = ar.alloc([4096]); r_cm = R()
            h.dma("sp", QA_sb, fm(I["QA"]), [], [r_qa])
            h.dma("sp", GA_sb, fm(I["GA"]), [], [r_ga])
            h.dma("sp", KA_sb, fm(KAh), [], [r_ka])
            h.dma("sp", RMK_sb, RMK, [], [r_rmk])
            h.dma("sp", Tg_sb, Tg, [], [r_tg])
            h.dma("sp", CM_sb, CMt, [], [r_cm])
            h.stt("dve", Tb.rearrange("p a b c -> p (a b c)"), Tg_sb, 8.0, CM_sb, ALU.mult, ALU.add, [r_tg, r_cm], [r_tb])
            h.memset("pool", VAe.rearrange("p a b c -> p (a b c)"), 1.0, [r_vae])
            for (src0, npr, dst0) in ((0, 11, 0), (64, 11, 11), (NHALO * 64, 2, 22)):
                h.dma("sp", VAt[:, 0:npr, :], VAh[src0:src0 + npr * 128, :].rearrange("(j p) c -> p j c", p=128), [r_vae], [r_vat])
                for j in range(npr):
                    h.copy("pool" if j % 2 else "dve", VAe[:, dst0 + j, :, 0:64],
                           VAt[:, j, :].rearrange("p (a b) -> p a b", a=8), [r_vat], [r_vae])
            sring = Ring(PS[0:3])
            oring = Ring(PS[3:5])
            CT0 = NHALO * 64
            pend = []

            def na_pv(item):
                i, h8, chunks, pt, ptr, oacc, oaccr = item
                n = len(chunks)
                for c, vidx in enumerate(chunks):
                    h.mm(oacc[:, h8 * 64:(h8 + 1) * 64], VAe[:, vidx, h8, :], pt[:, c * 64:(c + 1) * 64],
                         c == 0, c == n - 1, [r_vae, ptr], [oaccr])

            def na_fin(i, oacc, oaccr):
                tq = i * 64
                rc, rcr = rcring.next()
                h.recip(rc[64:128, :], oacc[64:128, :], [oaccr], [rcr])
                ov = oacc[0:64, :].rearrange("p (c two d) -> p c two d", c=4, two=2)
                rv = rc[64:128, :].rearrange("p (c two d) -> p c two d", c=4, two=2)
                h.tt("dve", oa_sb[0:64, :, tq:tq + 64], ov[:, :, 0, :], rv[:, :, 0, :], ALU.mult, [oaccr, rcr], [r_oa])
                h.tt("dve", oa_sb[64:128, :, tq:tq + 64], ov[:, :, 1, :], rv[:, :, 1, :], ALU.mult, [oaccr, rcr], [r_oa])
            fin_pend = []
            for i in range(20):
                tq = i * 64
                if i < 16:
                    if i <= 3:
                        rel0, npair, ls, ei = -4, 6, i, i
                    elif i <= 12:
                        rel0, npair, ls, ei = -4, 4, i, None
                    else:
                        rel0, npair, ls, ei = -8, 6, i - 4, 4 + (i - 13)
                else:
                    rel0, npair, ls, ei = 0, 0, 0, None
                oacc, oaccr = oring.next()
                for h8 in range(8):
                    hp = (h8 % 2) * 64; ch = h8 // 2
                    q = QA_sb[hp:hp + 64, ch, tq:tq + 64]
                    s, sr = sring.next()
                    chunks = []
                    for j in range(npair):
                        hr = ls + 2 * j
                        ptype = (rel0 + 2 * j + 8) // 2
                        h.mm(s[:, j * 64:(j + 1) * 64], KA_sb[hp:hp + 64, ch, hr * 64:hr * 64 + 128], q, True, False,
                             [r_ka, r_qa], [sr])
                        h.mm(s[:, j * 64:(j + 1) * 64], identb, Tb[:, h8, ptype, :], False, True, [r_cb, r_tb], [sr])
                        chunks.append(hr // 2 if hr % 2 == 0 else 11 + (hr - 1) // 2)
                    for cj in range(2):
                        c = npair + cj
                        h.mm(s[:, c * 64:(c + 1) * 64], KA_sb[hp:hp + 64, ch, CT0 + cj * 128:CT0 + (cj + 1) * 128], q, True, True,
                             [r_ka, r_qa], [sr])
                        chunks.append(22 + cj)
                    pt, ptr = ptring.next()
                    if ei is None:
                        w = (npair + 2) * 64
                        h.act(pt[:, 0:w], s[:, 0:w], AF.Exp, [sr], [ptr], scale=0.125)
                    else:
                        for j in range(npair):
                            col = ei * 6 + j
                            h.act(pt[:, j * 64:(j + 1) * 64], s[:, j * 64:(j + 1) * 64], AF.Exp, [sr, r_rmk], [ptr],
                                  bias=RMK_sb[:, col:col + 1], scale=0.125)
                        h.act(pt[:, npair * 64:(npair + 2) * 64], s[:, npair * 64:(npair + 2) * 64], AF.Exp, [sr], [ptr], scale=0.125)
                    pend.append((i, h8, chunks, pt, ptr, oacc, oaccr))
                    if len(pend) > 1:
                        it = pend.pop(0)
                        na_pv(it)
                        if it[1] == 7:
                            na_fin(it[0], it[5], it[6])
            while pend:
                it = pend.pop(0)
                na_pv(it)
                if it[1] == 7:
                    na_fin(it[0], it[5], it[6])
            h.tt("pool", oa_sb, oa_sb, GA_sb, ALU.mult, [r_oa, r_ga], [r_oa])
            h.dma("sp", fm(OUTS[0:512, :]), oa_sb, [r_oa], [r_OUTS[0]])
            P.barrier()
            ar.off = mark0

        if "pool" in phases:
            BI_sb = ar.alloc([4, BIW], BF16); r_bi = R()
            GB_sb = ar.alloc([4, NT], BF16); r_gb = R()
            ob_sb = ar.alloc([4, NT], BF16); r_ob = R()
            icn = ar.alloc([4, NT]); r_icn = R()
            pwf = ar.alloc([4, 128]); pwb = ar.alloc([4, 128], BF16); r_pw = R()
            psc_sb = ar.alloc([4]); r_psc = R()
            Sa = ar.alloc([NL + 16]); Sb = ar.alloc([NL + 16]); r_sa = R(); r_sb_ = R()
            dd = ar.alloc([NT], BF16); r_dd = R()
            h.dma("sp", BI_sb, fm(BIh), [], [r_bi])
            h.dma("sp", GB_sb, fm(I["GB"]), [], [r_gb])
            for gi in range(4):
                h.dma("sp", icn[:, gi, :], bcast_rows(ICN, gi, 0, NT), [], [r_icn])
            h.dma("sp", pwf, pool_w.rearrange("(g m) n -> m g n", m=128), [], [r_pw])
            h.copy("dve", pwb, pwf, [r_pw], [r_pw])
            h.dma("sp", psc_sb, psc, [], [r_psc])
            pring = Ring(PS[0:4])
            for gi in range(4):
                for (so, L, tok0) in ((0, NL, 0), (NL + 16, NCX, NL)):
                    W = L + 16
                    u = BI_sb[:, gi, so:so + W]
                    h.tt("dve", Sa[:, 1:W], u[:, 0:W - 1], u[:, 1:W], ALU.add, [r_bi], [r_sa])
                    cur, rcur, oth, roth = Sa, r_sa, Sb, r_sb_
                    lo, hi = 1, W
                    for lev in range(gi):
                        sft = 1 << lev
                        nlo, nhi = lo + sft, hi - sft
                        h.tt("dve", oth[:, nlo:nhi], cur[:, nlo - sft:nhi - sft], cur[:, nlo + sft:nhi + sft], ALU.add, [rcur], [roth])
                        cur, rcur, oth, roth = oth, roth, cur, rcur
                        lo, hi = nlo, nhi
                    assert lo <= 8 and hi >= 8 + L, (lo, hi, L)
                    h.tt("dve", oth[:, 8:8 + L], cur[:, 8:8 + L], icn[:, gi, tok0:tok0 + L], ALU.mult, [rcur, r_icn], [roth])
                    h.tt("dve", dd[:, tok0:tok0 + L], oth[:, 8:8 + L], u[:, 8:8 + L], ALU.subtract, [roth, r_bi], [r_dd])
                for (t0, tn) in TCH:
                    pt, pr = pring.next()
                    h.mm(pt[:, 0:tn], pwb[:, gi, :], dd[:, t0:t0 + tn], True, True, [r_pw, r_dd], [pr])
                    h.stt("dve", ob_sb[:, gi, t0:t0 + tn], pt[:, 0:tn], psc_sb[:, gi:gi + 1], GB_sb[:, gi, t0:t0 + tn],
                          ALU.mult, ALU.mult, [pr, r_psc, r_gb], [r_ob])
            h.dma("sp", fm(OUTS[512:1024, :]), ob_sb, [r_ob], [r_OUTS[1]])
            P.barrier()
            ar.off = mark0

        if "conv" in phases:
            U_sb = ar.alloc([4, UW], BF16); r_u = R()
            GD_sb = ar.alloc([4, NT], BF16); r_gd = R()
            od_sb = ar.alloc([4, NT], BF16); r_od = R()
            acc = ar.alloc([4, NT]); r_acc = [R() for _ in range(4)]
            yb = ar.alloc([4, NT], BF16); sqb = ar.alloc([4, NT], BF16); r_yb = R(); r_sqb = R()
            zb = ar.alloc([4, NT], BF16); r_zb = R()
            cw_sb = ar.alloc([4, 31]); r_cw = R()
            cv_sb = ar.alloc([4, 3]); r_cv = R()
            pcf = ar.alloc([4, 512]); pcb = ar.alloc([4, 512], BF16); r_pc = R()
            mean_sb = ar.alloc([512]); r_mean = R()
            m2 = ar.alloc([512]); r_m2 = R()
            rstd = ar.alloc([512]); r_rstd = R()
            tring = Ring([ar.alloc([512]) for _ in range(3)])
            h.dma("sp", U_sb, fm(Uh), [], [r_u])
            h.dma("sp", GD_sb, fm(I["GD"]), [], [r_gd])
            h.dma("sp", cw_sb.rearrange("p a b -> p (a b)"), cw, [], [r_cw])
            h.dma("sp", cv_sb.rearrange("p a b -> p (a b)"), cvec, [], [r_cv])
            h.dma("sp", pcf, conv_pw.rearrange("(m p) n -> p m n", p=128), [], [r_pc])
            h.copy("dve", pcb, pcf, [r_pc], [r_pc])
            for chn in range(4):
                eng = "dve"
                for (so, L, tok0) in ((0, NL, 0), (NL + 30, NCX, NL)):
                    a = acc[:, chn, tok0:tok0 + L]
                    h.ts(eng, a, U_sb[:, chn, so:so + L], cw_sb[:, chn, 0:1], cv_sb[:, chn, 0:1], ALU.mult, ALU.add,
                         [r_u, r_cw, r_cv], [r_acc[chn]])
                    for k in range(1, 31):
                        h.stt(eng, a, U_sb[:, chn, so + k:so + k + L], cw_sb[:, chn, k:k + 1], a, ALU.mult, ALU.add,
                              [r_u, r_cw, r_acc[chn]], [r_acc[chn]])
                h.copy("act", yb[:, chn, :], acc[:, chn, :], [r_acc[chn]], [r_yb])
                h.act(sqb[:, chn, :], acc[:, chn, :], AF.Square, [r_acc[chn]], [r_sqb])
            pring = Ring(PS[0:6])
            for (t0, tn) in TCH:
                pm, pmr = pring.next()
                pe2, pe2r = pring.next()
                for chn in range(4):
                    h.mm(pm[:, 0:tn], onesb, yb[:, chn, t0:t0 + tn], chn == 0, chn == 3, [r_cb, r_yb], [pmr])
                for chn in range(4):
                    h.mm(pe2[:, 0:tn], onesb, sqb[:, chn, t0:t0 + tn], chn == 0, chn == 3, [r_cb, r_sqb], [pe2r])
                h.copy("act", mean_sb[:, 0:tn], pm[:, 0:tn], [pmr], [r_mean])
                h.tt("dve", m2[:, 0:tn], mean_sb[:, 0:tn], mean_sb[:, 0:tn], ALU.mult, [r_mean], [r_m2])
                h.tt("dve", m2[:, 0:tn], pe2[:, 0:tn], m2[:, 0:tn], ALU.subtract, [pe2r, r_m2], [r_m2])
                h.act(rstd[:, 0:tn], m2[:, 0:tn], AF.Sqrt, [r_m2, r_eps], [r_rstd], bias=eps_c[:, 0:1], scale=1.0)
                h.recip(rstd[:, 0:tn], rstd[:, 0:tn], [r_rstd], [r_rstd])
                for chn in range(4):
                    t_, tr_ = tring.next()
                    h.tt("dve", t_[:, 0:tn], acc[:, chn, t0:t0 + tn], mean_sb[:, 0:tn], ALU.subtract, [r_acc[chn], r_mean], [tr_])
                    h.tt("pool", t_[:, 0:tn], t_[:, 0:tn], rstd[:, 0:tn], ALU.mult, [tr_, r_rstd], [tr_])
                    h.act(zb[:, chn, t0:t0 + tn], t_[:, 0:tn], AF.Silu, [tr_, r_cv], [r_zb],
                          bias=cv_sb[:, chn, 2:3], scale=cv_sb[:, chn, 1:2])
                for nch in range(4):
                    pt, pr = pring.next()
                    for mch in range(4):
                        h.mm(pt[:, 0:tn], pcb[:, mch, nch * 128:(nch + 1) * 128], zb[:, mch, t0:t0 + tn], mch == 0, mch == 3,
                             [r_pc, r_zb], [pr])
                    h.tt("dve", od_sb[:, nch, t0:t0 + tn], pt[:, 0:tn], GD_sb[:, nch, t0:t0 + tn], ALU.mult, [pr, r_gd], [r_od])
            h.dma("sp", fm(OUTS[3 * 512:4 * 512, :]), od_sb, [r_od], [r_OUTS[3]])
            P.barrier()
            ar.off = mark0

        ym = ar.alloc([KC, NT], BF16); r_ym = R()
        mark2 = ar.off
        if "merge" in phases:
            hT = ar.alloc([KC, NT], BF16); r_hT = R()
            outs = ar.alloc([16, NT], BF16); r_outs = R()
            h.dma("sp", hT, HT.rearrange("p (a b) -> p a b", a=KC), [], [r_hT])
            for b in range(4):
                h.dma("sp", outs[:, b * 4:(b + 1) * 4, :], fm(OUTS[b * 512:(b + 1) * 512, :]), [r_OUTS[b]], [r_outs])
            wmring = Ring([ar.alloc([KC, 512], BF16) for _ in range(2)])
            wbring = Ring([ar.alloc([4, 512], BF16) for _ in range(2)])
            sgring = Ring([ar.alloc([512]) for _ in range(2)])
            tpring = Ring([ar.alloc([512]) for _ in range(2)])
            acc4 = ar.alloc([4, NT])
            r_acc4 = [[R() for _ in TCH] for _ in range(4)]
            gring = Ring(PS[0:4])
            zring = Ring(PS[4:8])
            for dcg in range(4):
                for b in range(4):
                    wm, wmr = wmring.next()
                    wb, wbr = wbring.next()
                    c0 = b * D + dcg * 512
                    P.dma("pool", (lambda wm, c0: lambda e: e.dma_start(
                        out=wm, in_=w_m[:, c0:c0 + 512].rearrange("(kc p) n -> p kc n", p=128)))(wm, c0),
                        writes=[wmr])
                    P.dma("pool", (lambda wb, b, dcg: lambda e: e.dma_start(
                        out=wb, in_=w_br[b * 512:(b + 1) * 512, dcg * 512:(dcg + 1) * 512].rearrange("(j p) n -> p j n", p=128)))(wb, b, dcg),
                        writes=[wbr])
                    for q in range(4):
                        dc = dcg * 4 + q
                        for ci, (t0, tn) in enumerate(TCH):
                            gp, gpr = gring.next()
                            zp, zpr = zring.next()
                            for kc in range(KC):
                                h.mm(gp[:, 0:tn], wm[:, kc, q * 128:(q + 1) * 128], hT[:, kc, t0:t0 + tn], kc == 0, kc == KC - 1,
                                     [wmr, r_hT], [gpr])
                            for kq in range(4):
                                h.mm(zp[:, 0:tn], wb[:, kq, q * 128:(q + 1) * 128], outs[:, b * 4 + kq, t0:t0 + tn], kq == 0, kq == 3,
                                     [wbr, r_outs], [zpr])
                            sg, sgr = sgring.next()
                            h.act(sg[:, 0:tn], gp[:, 0:tn], AF.Sigmoid, [gpr], [sgr])
                            ac = acc4[:, q, t0:t0 + tn]
                            acr = r_acc4[q][ci]
                            if b == 0:
                                h.tt("dve", ac, sg[:, 0:tn], zp[:, 0:tn], ALU.mult, [sgr, zpr], [acr])
                            else:
                                tp, tpr = tpring.next()
                                h.tt("dve", tp[:, 0:tn], sg[:, 0:tn], zp[:, 0:tn], ALU.mult, [sgr, zpr], [tpr])
                                if b < 3:
                                    h.tt("pool", ac, ac, tp[:, 0:tn], ALU.add, [acr, tpr], [acr])
                                else:
                                    h.tt("pool", ym[:, dc, t0:t0 + tn], ac, tp[:, 0:tn], ALU.add, [acr, tpr], [r_ym])
            P.barrier()
            ar.off = mark2

        if "out" in phases:
            wo = ar.alloc([KC, D], BF16); r_wo = R()
            for kc in range(KC):
                P.dma("pool", (lambda kc: lambda e: e.dma_start(out=wo[:, kc, :], in_=w_out[kc * 128:(kc + 1) * 128, :]))(kc),
                      writes=[r_wo])
            G2 = [ar.alloc([D]) for _ in range(2)]; r_g2 = [R(), R()]
            gpo = ar.alloc([D]); r_gpo = R()
            h.dma("sp", gpo, bcast_rows(g_post, 0, 0, D), [], [r_gpo])
            for r in range(2):
                h.dma("sp", G2[r], bcast_rows(GATE, r, 0, D), [], [r_g2[r]])
                h.tt("dve", G2[r], G2[r], gpo, ALU.mult, [r_g2[r], r_gpo], [r_g2[r]])
            xring = Ring([ar.alloc([D]) for _ in range(2)])
            oring2 = Ring([ar.alloc([D]) for _ in range(2)])
            junk = ar.alloc([512], BF16); r_junk = R()
            stat = Ring([ar.alloc([8]) for _ in range(2)])
            t2ring = Ring([ar.alloc([512]) for _ in range(2)])
            bankset = [PS[0:4], PS[4:8]]
            bankr = [[R() for _ in range(4)], [R() for _ in range(4)]]
            for tt in range(NT // 128):
                r = 0 if tt < 8 else 1
                bs = bankset[tt % 2]; br = bankr[tt % 2]
                xt, xr = xring.next()
                h.dma("sp", xt, x[tt * 128:(tt + 1) * 128, :], [], [xr])
                st, sr = stat.next()
                for n in range(4):
                    for kc in range(KC):
                        h.mm(bs[n], ym[:, kc, tt * 128:(tt + 1) * 128], wo[:, kc, n * 512:(n + 1) * 512], kc == 0, kc == KC - 1,
                             [r_ym, r_wo], [br[n]])
                    h.act(junk, bs[n], AF.Square, [br[n]], [r_junk, sr], accum_out=st[:, n:n + 1])
                h.tt("dve", st[:, 4:5], st[:, 0:1], st[:, 1:2], ALU.add, [sr], [sr])
                h.tt("dve", st[:, 5:6], st[:, 2:3], st[:, 3:4], ALU.add, [sr], [sr])
                h.tt("dve", st[:, 4:5], st[:, 4:5], st[:, 5:6], ALU.add, [sr], [sr])
                h.act(st[:, 6:7], st[:, 4:5], AF.Sqrt, [sr, r_eps], [sr], bias=eps_c[:, 0:1], scale=1.0 / D)
                h.recip(st[:, 7:8], st[:, 6:7], [sr], [sr])
                ot, otr = oring2.next()
                for n in range(4):
                    t2, t2r = t2ring.next()
                    h.stt("dve", t2, bs[n], st[:, 7:8], G2[r][:, n * 512:(n + 1) * 512], ALU.mult, ALU.mult,
                          [br[n], sr, r_g2[r]], [t2r])
                    h.tt("pool", ot[:, n * 512:(n + 1) * 512], t2, xt[:, n * 512:(n + 1) * 512], ALU.add, [t2r, xr], [otr])
                h.dma("sp", XN[tt * 128:(tt + 1) * 128, :], ot, [otr], [])
        P.finish()
        P.build()
    return nc

bf16 = ml_dtypes.bfloat16
GRID_W = 64
def consts_for(q_norm_l, k_norm_l):
    ident = np.eye(128, dtype=np.float32)
    bd = np.zeros((128, 128), np.float32)
    bd[:64, :64] = 1.0 / 64; bd[64:, 64:] = 1.0 / 64
    rm = np.zeros((128, 128), np.float32)
    for m in range(128):
        if m % 32 < 16: rm[m + 16, m] = -1.0
        else: rm[m - 16, m] = 1.0
    qg = np.tile(q_norm_l, 2)[:, None].astype(np.float32)
    kg = np.tile(k_norm_l, 2)[:, None].astype(np.float32)
    return np.concatenate([ident, bd, rm, qg, kg], axis=1)
def cos_sin(core):
    t = 1024 * core + np.arange(1024)
    rows = (t // GRID_W).astype(np.float32); cols = (t % GRID_W).astype(np.float32)
    p = np.arange(128); d = p % 64; part = d // 32; j = d % 16
    freqs = (10000.0 ** (-(np.arange(16, dtype=np.float32)) / 16)).astype(np.float32)
    f = freqs[j]
    pos = np.where(part[:, None] == 0, rows[None, :], cols[None, :]).astype(np.float32)
    ang = (pos * f[:, None]).astype(np.float32)
    return np.concatenate([np.cos(ang), np.sin(ang)], axis=1).astype(np.float32)
def inputs_A(inp, l, core, ctx_l, x_c):
    c2 = np.stack([inp["c"][0], inp["c_ctx"]])
    c2T = np.ascontiguousarray(c2.reshape(2, 16, 128).transpose(2, 1, 0)).reshape(128, 32)
    return {
        "x": np.ascontiguousarray(np.concatenate([x_c, ctx_l], axis=0)),
        "c2T": c2T,
        "w_ada": inp["w_ada"][l], "b_ada2": np.ascontiguousarray(np.stack([inp["b_ada"][l]] * 2)),
        "g_pre": inp["g_pre"][l][None, :],
        "w_in": np.ascontiguousarray(inp["w_in"][l][:, :5888]),
        "consts": consts_for(inp["q_norm"][l], inp["k_norm"][l]),
        "cs": cos_sin(core),
    }

NEG = -30000.0
def to_bf(a): return np.ascontiguousarray(a).astype(bf16) if a.dtype != bf16 else np.ascontiguousarray(a)
def static_B():
    cq = np.arange(64); kc = np.arange(64)
    col_start = np.clip(cq - 8, 0, 48)
    col_ok = (kc[:, None] >= col_start[None, :]) & (kc[:, None] < col_start[None, :] + 16)
    cm = np.where(col_ok, 0.0, NEG * 8).astype(np.float32)
    cm128 = np.concatenate([cm, cm], axis=0)
    CMt = np.ascontiguousarray(np.tile(cm128[:, None, :], (1, 64, 1)).reshape(128, 4096))
    consts = np.concatenate([np.eye(128, dtype=np.float32), np.full((128, 128), 1.0 / 512, np.float32)], axis=1)
    return CMt, consts
def gather_T(rpb_l):
    cq = np.arange(64); kc = np.arange(64)
    co = np.clip(kc[:, None] - cq[None, :], -15, 15) + 15
    T = np.zeros((128, 8, 8, 64), np.float32)
    for h in range(8):
        for pt in range(8):
            rel_a = 2 * pt - 8
            for half in range(2):
                ro = rel_a + half + 7
                if 0 <= ro <= 14:
                    T[half * 64:(half + 1) * 64, h, pt, :] = rpb_l[h, ro][co]
    return np.ascontiguousarray(T.reshape(128, 4096))
def rowmask(core):
    M = np.zeros((128, 42), np.float32)
    for ei in range(7):
        i = ei if ei < 4 else 13 + (ei - 4)
        rel0 = -4 if ei < 4 else -8
        r = 16 * core + i
        blo = min(max(r - 4, 0), 120)
        for j in range(6):
            for half in range(2):
                g = r + rel0 + 2 * j + half
                ok = (0 <= g <= 127) and (blo <= g <= blo + 7)
                M[half * 64:(half + 1) * 64, ei * 6 + j] = 0.0 if ok else NEG
    return M
def invcnt(core):
    out = np.zeros((4, 1280), np.float32)
    for gi, k in enumerate((2, 4, 8, 16)):
        for (L, t, dst) in ((8192, 1024 * core + np.arange(1024), slice(0, 1024)), (256, np.arange(256), slice(1024, 1280))):
            lo = np.clip(t - k // 2, 0, L - 1); hi = np.clip(t + k - 1 - k // 2, 0, L - 1)
            out[gi, dst] = 1.0 / (hi - lo + 1).astype(np.float32)
    return out
def halo_cols(allarr, ctxarr, t_lo, t_hi, padc):
    F = allarr.shape[0]
    out = np.zeros((F, t_hi - t_lo), allarr.dtype)
    a, b = max(t_lo, 0), min(t_hi, allarr.shape[1])
    out[:, a - t_lo:b - t_lo] = allarr[:, a:b]
    if padc is None:
        return np.concatenate([out, ctxarr], axis=1)
    z = np.zeros((F, padc), allarr.dtype)
    return np.concatenate([out, z, ctxarr, z], axis=1)
def assemble_B(inp, l, Aout, xs, ctx_l, statics):
    CMt, constsB = statics
    cat = lambda k: np.concatenate([Aout[c][k][:, :1024] for c in range(8)], axis=1)
    catT = lambda k: np.concatenate([Aout[c][k][:1024] for c in range(8)], axis=0)
    KA_all = cat("KA"); BI_all = cat("BI"); U_all = cat("U"); KC_all = cat("KC")
    VA_all = catT("VA"); VC_all = catT("VC")
    A0 = Aout[0]
    KCall = np.ascontiguousarray(np.concatenate([KC_all, A0["KC"][:, 1024:]], axis=1))
    VCall = np.ascontiguousarray(np.concatenate([VC_all, A0["VC"][1024:]], axis=0))
    Tg = gather_T(inp["na_rpb"][l])
    cwl = inp["conv_w"][l]
    cw = np.ascontiguousarray(cwl.reshape(31, 4, 128).transpose(2, 1, 0)).reshape(128, 124)
    cv = np.stack([inp["conv_b"][l], inp["conv_ln_g"][l], inp["conv_ln_b"][l]], axis=1)
    cvec = np.ascontiguousarray(cv.reshape(4, 128, 3).transpose(1, 0, 2)).reshape(128, 12)
    common = {
        "g_post": inp["g_post"][l][None, :], "KCall": KCall, "VCall": VCall, "Tg": Tg, "CMt": CMt,
        "w_m": np.ascontiguousarray(inp["w_in"][l][:, 5888:]), "w_br": np.ascontiguousarray(inp["w_branch"][l].reshape(2048, 2048)),
        "w_out": inp["w_out"][l], "pool_w": np.ascontiguousarray(inp["pool_w"][l].reshape(512, 128)),
        "psc": np.ascontiguousarray(inp["pool_scale"][l].reshape(4, 128).T), "cw": cw, "cvec": cvec,
        "conv_pw": inp["conv_pw"][l], "consts": constsB,
    }
    ims = []
    for c in range(8):
        A = Aout[c]
        r0 = 16 * c
        d = dict(common)
        d["x"] = np.ascontiguousarray(np.concatenate([xs[c], ctx_l], axis=0))
        d["GATE"] = np.ascontiguousarray(A["MOD"][:, 4096:6144])
        d["HT"] = A["HT"]
        for k in ("QA", "QC", "GA", "GB", "GC", "GD"):
            d[k] = A[k]
        d["KAh"] = np.ascontiguousarray(halo_cols(KA_all, A0["KA"][:, 1024:], (r0 - 4) * 64, (r0 + 19) * 64, None))
        vah = halo_cols(np.ascontiguousarray(VA_all.T), np.ascontiguousarray(A0["VA"][1024:].T), (r0 - 4) * 64, (r0 + 19) * 64, None)
        d["VAh"] = np.ascontiguousarray(vah.T)
        d["BIh"] = np.ascontiguousarray(halo_cols(BI_all, A0["BI"][:, 1024:], 1024 * c - 8, 1024 * c + 1024 + 8, 8))
        d["Uh"] = np.ascontiguousarray(halo_cols(U_all, A0["U"][:, 1024:], 1024 * c - 15, 1024 * c + 1024 + 15, 15))
        d["RMK"] = rowmask(c)
        d["ICN"] = invcnt(c)
        ims.append(d)
    return ims

_CACHE = {}


def kernel(**inputs):
    inp = {k: np.asarray(v) for k, v in inputs.items()}
    if "A" not in _CACHE:
        _CACHE["A"] = build_A()
        _CACHE["B"] = build_B()
        _CACHE["S"] = static_B()
    ncA, ncB, statics = _CACHE["A"], _CACHE["B"], _CACHE["S"]
    cores = list(range(8))
    xs = [np.ascontiguousarray(inp["x"][0, 1024 * c:1024 * (c + 1)]) for c in cores]
    ctx_l = np.ascontiguousarray(inp["ctx"][0])
    for l in range(4):
        imsA = [inputs_A(inp, l, c, ctx_l, xs[c]) for c in cores]
        resA = run_bass_kernel_spmd(ncA, imsA, core_ids=cores)
        Aout = [{k: np.asarray(v) for k, v in resA.results[c].items()} for c in cores]
        imsB = assemble_B(inp, l, Aout, xs, ctx_l, statics)
        resB = run_bass_kernel_spmd(ncB, imsB, core_ids=cores)
        XN = [np.asarray(resB.results[c]["XN"]) for c in cores]
        xs = [np.ascontiguousarray(XN[c][:1024]) for c in cores]
        ctx_l = np.ascontiguousarray(XN[0][1024:])
    return np.concatenate(xs, axis=0)[None].astype(np.float32)
```
